# Optimizing a Trainium2 kernel written in Bass

```python
import jax, jax.numpy as jnp
from jax import lax
import numpy as np

D_MODEL = 4096
BATCH = 1
SEQ = 8192
DEPTH = 4

MEM_LEN = 256
N_MIXERS = 2
MIX_WIDTH = D_MODEL
MEM_HEADS = 4
MEM_WIDTH = MIX_WIDTH // 4
MEM_HEAD_DIM = MEM_WIDTH // MEM_HEADS
TOK_WIDTH = MIX_WIDTH - MEM_WIDTH
POOL_WINDOWS = (2, 4, 8, 16)
POOL_GROUP = TOK_WIDTH // len(POOL_WINDOWS)
RET_HEAD_DIM = 256
RET_HEADS = TOK_WIDTH // RET_HEAD_DIM
RET_CHUNK = 128
D_FF = 4 * D_MODEL
ROPE_BASE = 10000.0
EPS = 1e-6

kernel_name = "hybrid_pool_retention_memory_trunk"


def rmsnorm(x, g):
    xf = x.astype(jnp.float32)
    y = xf * lax.rsqrt(jnp.mean(xf * xf, axis=-1, keepdims=True) + EPS)
    return (y * g.astype(jnp.float32)).astype(x.dtype)


def rope(t, positions):
    half = t.shape[-1] // 2
    inv = ROPE_BASE ** (-jnp.arange(half, dtype=jnp.float32) / half)
    ang = positions.astype(jnp.float32)[:, :, None] * inv
    cos = jnp.cos(ang)[:, :, None, :]
    sin = jnp.sin(ang)[:, :, None, :]
    tf = t.astype(jnp.float32)
    t1, t2 = tf[..., :half], tf[..., half:]
    return jnp.concatenate([t1 * cos - t2 * sin, t1 * sin + t2 * cos], axis=-1).astype(t.dtype)


def multiscale_pool(u, w_grp, scale):
    B, S, _ = u.shape
    uf = u.astype(jnp.float32)
    csum = jnp.pad(jnp.cumsum(uf, axis=1), ((0, 0), (1, 0), (0, 0)))
    t = jnp.arange(S)
    outs = []
    for g, w in enumerate(POOL_WINDOWS):
        lo, hi = g * POOL_GROUP, (g + 1) * POOL_GROUP
        cg = csum[..., lo:hi]
        start = jnp.maximum(t + 1 - w, 0)
        win_sum = cg[:, 1:] - cg[:, start]
        count = (t + 1 - start).astype(jnp.float32)
        outs.append(win_sum / count[None, :, None] - uf[..., lo:hi])
    p = jnp.stack(outs, axis=2).astype(u.dtype)
    y = jnp.einsum('bsgc,gcd->bsgd', p, w_grp).reshape(B, S, TOK_WIDTH)
    return y * scale


def retention(q, k, v, gate, positions):
    B, S, H, Dh = q.shape
    C = RET_CHUNK
    nc = S // C
    q = rope(q, positions)
    k = rope(k, positions) * (Dh ** -0.5)
    log_g = jnp.log1p(-(2.0 ** (-5.0 - jnp.arange(H, dtype=jnp.float32))))
    idx = jnp.arange(C, dtype=jnp.float32)
    diff = idx[:, None] - idx[None, :]
    decay_in = jnp.where(diff >= 0, jnp.exp(log_g[:, None, None] * jnp.maximum(diff, 0.0)), 0.0)
    xi = jnp.exp(log_g[None, :] * (idx + 1.0)[:, None])
    zeta = jnp.exp(log_g[None, :] * (C - 1.0 - idx)[:, None])
    chunk_decay = jnp.exp(log_g * C)

    qc = q.reshape(B, nc, C, H, Dh)
    kc = k.reshape(B, nc, C, H, Dh)
    vc = v.reshape(B, nc, C, H, Dh)
    scores = jnp.einsum('bnthd,bnshd->bnhts', qc, kc) * decay_in.astype(q.dtype)
    inner = jnp.einsum('bnhts,bnshd->bnthd', scores, vc)

    def step(state, inp):
        qi, ki, vi = inp
        cross = jnp.einsum('bthd,bhde->bthe', qi * xi[None, :, :, None], state)
        state = state * chunk_decay[None, :, None, None] + jnp.einsum(
            'bshd,bshe->bhde', ki * zeta[None, :, :, None], vi)
        return state, cross

    state0 = jnp.zeros((B, H, Dh, Dh), jnp.float32)
    _, cross = lax.scan(step, state0, (qc.swapaxes(0, 1), kc.swapaxes(0, 1), vc.swapaxes(0, 1)))
    out = (inner.astype(jnp.float32) + cross.swapaxes(0, 1)).reshape(B, S, H, Dh)
    out = out * lax.rsqrt(jnp.mean(out * out, axis=-1, keepdims=True) + EPS)
    out = out.reshape(B, S, H * Dh).astype(gate.dtype)
    return jax.nn.silu(gate) * out


def memory_attention(mq, mem_k, mem_v):
    B, S = mq.shape[:2]
    s = jnp.einsum('bshd,bmhd->bhsm', mq, mem_k).astype(jnp.float32) * (MEM_HEAD_DIM ** -0.5)
    p = jax.nn.softmax(s, axis=-1).astype(mq.dtype)
    return jnp.einsum('bhsm,bmhd->bshd', p, mem_v).reshape(B, S, MEM_WIDTH)


def setup_inputs(seed: int = 0) -> dict:
    key = jax.random.key(seed)
    ks = jax.random.split(key, 20)
    n_pool = (DEPTH + 1) // 2
    n_ret = DEPTH // 2
    f32 = jnp.float32
    nrm = lambda k, shape, fan: jax.random.normal(k, shape, f32) * (fan ** -0.5)
    gain = lambda k, shape: 1.0 + 0.01 * jax.random.normal(k, shape, f32)
    offset = jax.random.randint(ks[2], (BATCH, 1), 0, 1024, dtype=jnp.int32)
    return {
        "x": jax.random.normal(ks[0], (BATCH, SEQ, D_MODEL), f32),
        "mem": jax.random.normal(ks[1], (BATCH, MEM_LEN, D_MODEL), f32),
        "positions": offset + jnp.arange(SEQ, dtype=jnp.int32)[None, :],
        "mix_norm_g": gain(ks[3], (DEPTH, D_MODEL)),
        "w_in_pool": nrm(ks[4], (n_pool, D_MODEL, TOK_WIDTH + MEM_WIDTH), D_MODEL),
        "w_pool_grp": nrm(ks[5], (n_pool, len(POOL_WINDOWS), POOL_GROUP, POOL_GROUP), POOL_GROUP),
        "pool_scale": 1.0 + 0.1 * jax.random.normal(ks[6], (n_pool, TOK_WIDTH), f32),
        "w_out_pool": nrm(ks[7], (n_pool, MIX_WIDTH, D_MODEL), MIX_WIDTH),
        "w_in_ret": nrm(ks[8], (n_ret, D_MODEL, 4 * TOK_WIDTH + MEM_WIDTH), D_MODEL),
        "w_out_ret": nrm(ks[9], (n_ret, MIX_WIDTH, D_MODEL), MIX_WIDTH),
        "mem_norm_g": gain(ks[10], (D_MODEL,)),
        "w_mem_kv": nrm(ks[11], (D_MODEL, 2 * MEM_WIDTH), D_MODEL),
        "ffn_norm_g": gain(ks[12], (DEPTH, D_MODEL)),
        "w_up": nrm(ks[13], (DEPTH, D_MODEL, D_FF), D_MODEL),
        "w_down": nrm(ks[14], (DEPTH, D_FF, D_MODEL), D_FF),
        "final_norm_g": gain(ks[15], (D_MODEL,)),
    }


def reference(x, mem, positions, mix_norm_g, w_in_pool, w_pool_grp, pool_scale, w_out_pool,
              w_in_ret, w_out_ret, mem_norm_g, w_mem_kv, ffn_norm_g, w_up, w_down, final_norm_g):
    B, S, _ = x.shape
    M = mem.shape[1]
    mem_kv = rmsnorm(mem, mem_norm_g) @ w_mem_kv
    mem_k = mem_kv[..., :MEM_WIDTH].reshape(B, M, MEM_HEADS, MEM_HEAD_DIM)
    mem_v = mem_kv[..., MEM_WIDTH:].reshape(B, M, MEM_HEADS, MEM_HEAD_DIM)

    for i in range(DEPTH):
        j = i // N_MIXERS
        h = rmsnorm(x, mix_norm_g[i])
        if i % N_MIXERS == 0:
            u = h @ w_in_pool[j]
            tok_out = multiscale_pool(u[..., :TOK_WIDTH], w_pool_grp[j], pool_scale[j])
            mq = u[..., TOK_WIDTH:]
            w_out = w_out_pool[j]
        else:
            u = h @ w_in_ret[j]
            hs = (B, S, RET_HEADS, RET_HEAD_DIM)
            q = u[..., 0 * TOK_WIDTH:1 * TOK_WIDTH].reshape(hs)
            k = u[..., 1 * TOK_WIDTH:2 * TOK_WIDTH].reshape(hs)
            v = u[..., 2 * TOK_WIDTH:3 * TOK_WIDTH].reshape(hs)
            g = u[..., 3 * TOK_WIDTH:4 * TOK_WIDTH]
            tok_out = retention(q, k, v, g, positions)
            mq = u[..., 4 * TOK_WIDTH:]
            w_out = w_out_ret[j]
        mem_out = memory_attention(mq.reshape(B, S, MEM_HEADS, MEM_HEAD_DIM), mem_k, mem_v)
        x = x + jnp.concatenate([tok_out, mem_out], axis=-1) @ w_out
        h = rmsnorm(x, ffn_norm_g[i])
        x = x + jnp.square(jax.nn.relu(h @ w_up[i])) @ w_down[i]
    return rmsnorm(x, final_norm_g)
```

```python
import numpy as np
import concourse.bass as bass
import concourse.mybir as mybir
from concourse.bass_utils import run_bass_kernel_spmd

F32 = mybir.dt.float32
BF16 = mybir.dt.bfloat16
I32 = mybir.dt.int32
ALU = mybir.AluOpType
AF = mybir.ActivationFunctionType
AX = mybir.AxisListType
EPS = 1e-6
HALO = 16
PI = float(np.pi)


class Cfg:
    def __init__(s, D=4096, TL=1024, TB=512, M=256, MEMH=4, DFF=16384, NCORES=8):
        s.D, s.TL, s.TB, s.M, s.MEMH, s.DFF, s.NCORES = D, TL, TB, M, MEMH, DFF, NCORES
        s.DC = D // 128
        s.MEMW = MEMH * 256
        s.MEMC = s.MEMW // 128
        s.TOK = D - s.MEMW
        s.TOKC = s.TOK // 128
        s.G = s.TOK // 4
        s.GC = s.G // 128
        s.RH = s.TOK // 256
        s.NTC = TB // 128
        s.NBLK = TL // TB
        s.MC = M // 128
        s.FB = 16
        s.NFB = DFF // (128 * s.FB)
        s.NCH = TL // 128
        s.INW = 4 * s.TOK + s.MEMW


class Buf:
    def __init__(self, name):
        self.name = name
        self.wr = {}
        self.rd = {}


class Ctx:
    def __init__(self, nc):
        self.nc = nc
        self.eng = {"pe": nc.tensor, "act": nc.scalar, "dve": nc.vector, "pool": nc.gpsimd, "sp": nc.sync}
        self.sem = {k: nc.alloc_semaphore("s_" + k) for k in ("pe", "act", "dve")}
        self.cnt = {k: 0 for k in self.sem}
        self.seen = {k: {} for k in self.eng}
        self.dslots = {}
        for q in ("pool", "sp"):
            self.dslots[q] = []
            for i in range(6):
                key = f"d_{q}{i}"
                self.sem[key] = nc.alloc_semaphore("s_" + key)
                self.cnt[key] = 0
                self.dslots[q].append(key)
        self.drr = {"pool": 0, "sp": 0}
        self.allbufs = []

    def buf(self, name):
        b = Buf(name)
        self.allbufs.append(b)
        return b

    def _need(self, e, dep):
        if dep is None:
            return
        key, c = dep
        if key == e and e == "pe":
            return
        if self.seen[e].get(key, 0) >= c:
            return
        self.seen[e][key] = c
        self.eng[e].wait_ge(self.sem[key], c)

    def acquire(self, e, reads=(), writes=()):
        for b in reads:
            for key, c in b.wr.items():
                self._need(e, (key, c))
        for b in writes:
            for key, c in b.wr.items():
                self._need(e, (key, c))
            for key, c in b.rd.items():
                self._need(e, (key, c))

    def _record(self, key, reads, writes):
        c = self.cnt[key]
        for b in reads:
            if b in writes:
                continue
            b.rd[key] = c
        for b in writes:
            b.wr[key] = c
            b.rd = {}

    def release(self, e, ins, reads=(), writes=()):
        ins.then_inc(self.sem[e], 1)
        self.cnt[e] += 1
        self._record(e, reads, writes)

    def op(self, e, fn, reads=(), writes=()):
        self.acquire(e, reads, writes)
        ins = fn(self.eng[e])
        self.release(e, ins, reads, writes)
        return ins

    def dma(self, q, out, in_, reads=(), writes=()):
        key = self.dslots[q][self.drr[q] % len(self.dslots[q])]
        self.drr[q] += 1
        if self.cnt[key] > 0:
            self._need(q, (key, self.cnt[key]))
        self.acquire(q, reads, writes)
        ins = self.eng[q].dma_start(out=out, in_=in_)
        ins.then_inc(self.sem[key], 16)
        self.cnt[key] += 16
        self._record(key, reads, writes)
        return ins

    def inherit(self, dsts, srcs):
        for d in dsts:
            for s_ in srcs:
                if s_ is d:
                    continue
                for key, c in list(s_.rd.items()) + list(s_.wr.items()):
                    if d.rd.get(key, 0) < c:
                        d.rd[key] = c

    def finish(self):
        for e in ("sp",):
            for key, c in self.cnt.items():
                if c > 0:
                    self.eng[e].wait_ge(self.sem[key], c)


class Prog:
    def __init__(s, cfg, kind, cds, load_mem=False):
        s.c = cfg
        s.kind = kind
        s.load_mem = load_mem
        s.cds = cds
        s.planning = True
        s.tiles = []
        s.build()

    def sb(s, name, shape, dt):
        n = int(np.prod(shape[1:])) * (4 if dt in (F32, I32) else 2)
        off = (s.sboff + 31) // 32 * 32
        s.sboff = off + n
        assert s.sboff <= 229300, (name, s.sboff)
        return s.nc.alloc_sbuf_tensor_at(name, list(shape), dt, offset=off)

    def sb_at(s, name, shape, dt, off):
        return s.nc.alloc_sbuf_tensor_at(name, list(shape), dt, offset=off)

    def dram_in(s, name, shape, dt=F32):
        return s.nc.dram_tensor(name, list(shape), dt, kind="ExternalInput").ap()

    def dram_out(s, name, shape, dt=F32):
        return s.nc.dram_tensor(name, list(shape), dt, kind="ExternalOutput").ap()

    def build(s):
        for planning in (True, False):
            s.planning = planning
            s.nc = bass.Bass("TRN2", target_bir_lowering=False)
            s.cx = Ctx(s.nc)
            s.sboff = 16512
            s.tile_i = 0
            s.tile_issued = 0
            if planning:
                s.tiles = []
            s.setup()
            s.body()
            if not planning:
                s.cx.finish()

    def setup(s):
        c, nc, cx = s.c, s.nc, s.cx
        A = s.kind == "A"
        d = s.d = {}
        d["xT"] = s.dram_in("xT", [c.D, HALO + c.TL])
        if s.load_mem:
            d["KT_i"] = s.dram_in("KT_i", [128, c.MEMC * c.M], BF16)
            d["V_i"] = s.dram_in("V_i", [128, c.MC * c.MEMW], BF16)
        else:
            d["memT"] = s.dram_in("memT", [c.D, c.M])
            d["KT_o"] = s.dram_out("KT_o", [128, c.MEMC * c.M], BF16)
            d["V_o"] = s.dram_out("V_o", [128, c.MC * c.MEMW], BF16)
        d["gcol"] = s.dram_in("gcol", [128, 4 * c.DC])
        d["ident"] = s.dram_in("ident", [128, 128])
        if not s.load_mem:
            d["w_mem_kv"] = s.dram_in("w_mem_kv", [c.D, 2 * c.MEMW])
        d["pos"] = s.dram_in("pos", [128, c.TL], I32)
        d["invf"] = s.dram_in("invf", [128, 1])
        d["zeta16"] = s.dram_in("zeta16", [128, c.RH])
        d["xi"] = s.dram_in("xi", [128, c.RH * 128])
        d["maskT"] = s.dram_in("maskT", [128, c.RH * 128])
        d["w_out"] = s.dram_in("w_out", [c.D, c.D])
        d["w_up"] = s.dram_in("w_up", [c.D, c.DFF])
        d["w_down"] = s.dram_in("w_down", [c.DFF, c.D])
        d["xT_out"] = s.dram_out("xT_out", [c.D, c.TL])
        if A:
            d["w_in"] = s.dram_in("w_in", [c.D, c.D])
            d["w_grp"] = s.dram_in("w_grp", [4 * c.G, c.G])
            d["pscale"] = s.dram_in("pscale", [128, c.TOKC])
            d["invc"] = s.dram_in("invc", [128, 4 * 16])
            d["w_in2"] = s.dram_in("w_in2", [c.D, c.INW])
            d["Lout"] = s.dram_out("Lout", [c.RH * 2 * 128, 256])
            d["kz_o"] = s.dram_out("kz_o", [c.RH * c.NBLK * 128, c.NTC * 256], BF16)
            d["v_o"] = s.dram_out("v_o", [c.RH * c.NBLK * 128, c.NTC * 256], BF16)
            d["kr_o"] = s.dram_out("kr_o", [c.RH * c.NBLK * 128, 2 * c.TB], BF16)
        else:
            d["w_in"] = s.dram_in("w_in", [c.D, c.INW])
            d["Lall"] = s.dram_in("Lall", [c.NCORES * c.RH * 2 * 128, 256])
            d["wS"] = s.dram_in("wS", [128, c.NCORES * c.RH])
            d["kz_i"] = s.dram_in("kz_i", [c.RH * c.NBLK * 128, c.NTC * 256], BF16)
            d["v_i"] = s.dram_in("v_i", [c.RH * c.NBLK * 128, c.NTC * 256], BF16)
            d["kr_i"] = s.dram_in("kr_i", [c.RH * c.NBLK * 128, 2 * c.TB], BF16)
            d["xnT_out"] = s.dram_out("xnT_out", [c.D, c.TL])

        TB, DC = c.TB, c.DC
        t = s.t = {}
        b = s.b = {}
        s.r1 = s.sboff
        t["xblk"] = s.sb("xblk", [128, DC, TB], F32)
        r1 = b["xblk"] = cx.buf("r1")
        s.hT_off = (s.sboff + 31) // 32 * 32
        t["hT"] = s.sb("hT", [128, DC, TB], BF16)
        t["xnv"] = s.sb_at("xnv", [128, DC // 2, TB], F32, s.hT_off)
        b["hT"] = cx.buf("hT")
        s.catT_off = (s.sboff + 31) // 32 * 32
        t["catT"] = s.sb("catT", [128, DC, TB], BF16)
        b["catT"] = cx.buf("catT")
        s.NWB = 4
        t["wb"] = [s.sb(f"wb{i}", [128, 4096], BF16) for i in range(s.NWB)]
        b["wb"] = [cx.buf(f"wb{i}") for i in range(s.NWB)]
        t["rstd"] = s.sb("rstd", [128, TB], F32)
        b["rstd"] = cx.buf("rstd")
        t["ones"] = s.sb("ones", [128, 128], BF16)
        t["ident"] = s.sb("ident", [128, 128], F32)
        t["gcol"] = s.sb("gcol", [128, 4 * DC], F32)
        t["KT"] = s.sb("KT", [128, c.MEMC, c.M], BF16)
        t["V"] = s.sb("V", [128, c.MC, c.MEMW], BF16)
        b["KT"] = cx.buf("KT")
        b["V"] = cx.buf("V")
        b["const"] = cx.buf("const")
        t["state"] = s.sb("state", [128, c.RH, 2, 256], F32)
        b["state"] = [cx.buf(f"state{h}") for h in range(c.RH)]
        t["zeta16"] = s.sb("zeta16", [128, c.RH], F32)
        t["invf"] = s.sb("invf", [128, 1], F32)
        t["small"] = s.sb("small", [128, 16], F32)
        b["small"] = cx.buf("small")
        t["relu"] = s.sb("relu", [128, TB], F32)
        b["relu"] = cx.buf("relu")
        if A:
            t["pscale"] = s.sb("pscale", [128, c.TOKC], F32)
            t["invc"] = s.sb("invc", [128, 4, 16], F32)
            t["uhalo"] = s.sb("uhalo", [128, c.TOKC, 16], F32)
            b["uhalo"] = cx.buf("uhalo")
        else:
            t["wS"] = s.sb("wS", [128, c.NCORES * c.RH], F32)
        s.ps = [nc.alloc_psum_tensor(f"ps{i}", [128, 512], F32) for i in range(8)]
        s.psb = [cx.buf(f"ps{i}") for i in range(8)]
        s.bank_rr = 0
        s.grp_rr = 0

        o = s.r1
        s.tmpbufs = []

        def carve(name, shape, dt, at=None):
            nonlocal o
            if at is not None:
                o = at
            n = int(np.prod(shape[1:])) * (4 if dt in (F32, I32) else 2)
            o = (o + 31) // 32 * 32
            h = s.sb_at(name, list(shape), dt, o)
            o += n
            assert o <= s.r1 + DC * TB * 4, (name, o - s.r1)
            t[name] = h
            b[name] = cx.buf(name)
            s.tmpbufs.append(b[name])
            return h
        W = HALO + TB
        carve("cos", [128, TB], F32)
        carve("sin", [128, TB], F32)
        carve("posi", [128, TB], I32)
        carve("posf", [128, TB], F32)
        if not A:
            carve("xi", [128, c.RH, 128], F32)
            carve("maskT", [128, c.RH, 128], F32)
        o_t = o
        carve("mqT", [128, c.MEMC, TB], BF16)
        carve("pexp", [128, 2, c.M], F32)
        carve("pn", [128, 2, c.M], F32)
        carve("pT", [128, c.MC, TB], BF16)
        if A:
            o_pool = o
            carve("ugrp", [128, c.GC, W], F32)
            carve("ugrp1", [128, c.GC, W], F32)
            carve("pgrp", [128, c.GC, TB], BF16)
            carve("sa", [128, W], F32)
            carve("sbb", [128, W], F32)
            carve("t16", [128, 16], F32)
            o = o_pool
            carve("kT", [128, 2, TB], F32)
            carve("kr", [128, 2, TB], F32)
            carve("kT1", [128, 2, TB], F32)
            carve("kr1", [128, 2, TB], F32)
        else:
            o = o_t
        carve("ra", [128, TB], F32)
        carve("rb", [128, TB], F32)
        carve("vtm", [128, c.NTC, 256], BF16)
        carve("kz", [128, c.NTC, 256], BF16)
        carve("krb", [128, 2, TB], BF16)
        if A:
            carve("vtm1", [128, c.NTC, 256], BF16)
            carve("kz1", [128, c.NTC, 256], BF16)
            carve("krb1", [128, 2, TB], BF16)
        if not A:
            carve("qT", [128, 2, TB], F32)
            carve("qr", [128, 2, TB], BF16)
            carve("qxi", [128, 2, TB], BF16)
            carve("sg", [128, 2, 2, TB], F32)
            carve("sT", [128, c.NTC, 128], BF16)
            carve("stb", [128, 2, 256], BF16)
            carve("o", [128, 2, TB], F32)
            carve("osq", [128, 2, TB], BF16)

        s.membufs = [b[n] for n in ("mqT", "pexp", "pn", "pT")]
        b["pexp"] = [b["pexp"], cx.buf("pexp1")]
        b["pn"] = [b["pn"], cx.buf("pn1")]
        b["small"] = [b["small"], cx.buf("small1")]
        s.membufs += [b["pexp"][1], b["pn"][1]]
        s.tmpbufs += [b["pexp"][1], b["pn"][1]]
        if A:
            t["ugrp"] = [t["ugrp"], t["ugrp1"]]
            b["ugrp"] = [b["ugrp"], b["ugrp1"]]
        if not A:
            s.LstT = [s.sb_at(f"Lst{i}", [128, c.NCORES, 2, 256], F32, s.catT_off + i * c.NCORES * 2 * 256 * 4) for i in range(2)]
            s.LstB = [cx.buf("LstA"), cx.buf("LstB")]
            b["sg"] = [b["sg"], cx.buf("sg1")]
            s.tmpbufs.append(b["sg"][1])
            b["sT"] = [b["sT"]] + [cx.buf(f"sT{n}") for n in range(1, c.NTC)]
            b["o"] = [b["o"]] + [cx.buf(f"o{n}") for n in range(1, c.NTC)]
            s.tmpbufs += b["sT"][1:] + b["o"][1:]
        if s.planning:
            return
        cb = b["const"]
        cx.op("dve", lambda e: e.memset(t["ones"][:], 1.0), writes=[cb])
        cx.dma("sp", t["ident"][:], d["ident"], writes=[cb])
        cx.dma("sp", t["gcol"][:], d["gcol"], writes=[cb])
        cx.dma("sp", t["zeta16"][:], d["zeta16"], writes=[cb])
        cx.dma("sp", t["invf"][:], d["invf"], writes=[cb])
        if A:
            cx.dma("sp", t["pscale"][:], d["pscale"], writes=[cb])
            cx.dma("sp", t["invc"][:], d["invc"].rearrange("p (g t) -> p g t", t=16), writes=[cb])
        else:
            cx.dma("sp", t["wS"][:], d["wS"], writes=[cb])

    def rope_tables(s, blk, with_xi):
        if s.planning:
            return
        c, cx, t, b, d = s.c, s.cx, s.t, s.b, s.d
        TB = c.TB
        cb = b["const"]
        cx.dma("sp", t["posi"][:], d["pos"][:, blk * TB:(blk + 1) * TB], writes=[b["posi"]])
        if with_xi:
            cx.dma("sp", t["xi"][:], d["xi"].rearrange("p (h t) -> p h t", t=128), writes=[b["xi"]])
            cx.dma("sp", t["maskT"][:], d["maskT"].rearrange("p (h t) -> p h t", t=128), writes=[b["maskT"]])
        posf, posi, ra, rb = t["posf"], t["posi"], t["ra"], t["rb"]
        bpi, bpf, bra, brb = b["posi"], b["posf"], b["ra"], b["rb"]
        cx.op("dve", lambda e: e.tensor_copy(out=posf[:], in_=posi[:]), reads=[bpi], writes=[bpf])
        for name, off in (("sin", 0.0), ("cos", 0.25)):
            o_, ob = t[name], b[name]
            cx.op("dve", lambda e: e.tensor_scalar(out=o_[:], in0=posf[:], scalar1=t["invf"][:, 0:1], scalar2=off,
                                                   op0=ALU.mult, op1=ALU.add), reads=[bpf, cb], writes=[ob])
            cx.op("dve", lambda e: e.tensor_copy(out=posi[:], in_=o_[:]), reads=[ob], writes=[bpi])
            cx.op("dve", lambda e: e.tensor_copy(out=ra[:], in_=posi[:]), reads=[bpi], writes=[bra])
            cx.op("dve", lambda e: e.tensor_tensor(out=o_[:], in0=o_[:], in1=ra[:], op=ALU.subtract), reads=[ob, bra], writes=[ob])
            cx.op("dve", lambda e: e.tensor_scalar(out=rb[:], in0=o_[:], scalar1=0.5, scalar2=None, op0=ALU.is_gt),
                  reads=[ob], writes=[brb])
            cx.op("dve", lambda e: e.tensor_tensor(out=o_[:], in0=o_[:], in1=rb[:], op=ALU.subtract), reads=[ob, brb], writes=[ob])
            cx.op("dve", lambda e: e.tensor_scalar(out=rb[:], in0=o_[:], scalar1=-0.5, scalar2=None, op0=ALU.is_lt),
                  reads=[ob], writes=[brb])
            cx.op("dve", lambda e: e.tensor_tensor(out=o_[:], in0=o_[:], in1=rb[:], op=ALU.add), reads=[ob, brb], writes=[ob])
            cx.op("act", lambda e: e.activation(out=o_[:], in_=o_[:], func=AF.Sin, scale=6.28318), reads=[ob], writes=[ob])

    def banks(s, n):
        if n == 4:
            g = s.grp_rr % 2
            s.grp_rr += 1
            idx = list(range(4 * g, 4 * g + 4))
        else:
            idx = []
            for _ in range(n):
                idx.append(s.bank_rr % 8)
                s.bank_rr += 1
        return idx

    def issue_tiles(s, upto):
        cx = s.cx
        while s.tile_issued < min(upto, len(s.tiles)):
            i = s.tile_issued
            W, k0, KCt, c0, CW = s.tiles[i]
            slot = i % s.NWB
            dst = s.t["wb"][slot][:, 0:KCt * CW].rearrange("p (k c) -> p k c", c=CW)
            src = W[k0 * 128:(k0 + KCt) * 128, c0:c0 + CW].rearrange("(k p) c -> p k c", p=128)
            cx.dma("pool", dst, src, writes=[s.b["wb"][slot]])
            s.tile_issued += 1

    def gemm(s, W, k0, KC, c0, ncols, act, T, epi, abufs, tm=False):
        cx, nc = s.cx, s.nc
        KT_ = 8 if (KC >= 8 and KC % 8 == 0) else KC
        assert KC % KT_ == 0
        nkh = KC // KT_
        col = 0
        while col < ncols:
            CW = min(512, ncols - col)
            tiles = []
            for kh in range(nkh):
                if s.planning:
                    s.tiles.append((W, k0 + kh * KT_, KT_, c0 + col, CW))
                tiles.append(s.tile_i)
                s.tile_i += 1
            if s.planning:
                col += CW
                continue
            nacc = (CW // 128) if not tm else (T // 128)
            bidx = s.banks(4) if nacc > 2 else s.banks(nacc)
            bidx = bidx[:nacc]
            pb = [s.psb[i] for i in bidx]
            for kh in range(nkh):
                ti = tiles[kh]
                s.issue_tiles(ti + s.NWB)
                slot = ti % s.NWB
                wbuf = s.b["wb"][slot]
                wt = s.t["wb"][slot][:, 0:KT_ * CW].rearrange("p (k c) -> p k c", c=CW)
                cx.acquire("pe", reads=[wbuf] + abufs, writes=pb if kh == 0 else [])
                ins = None
                for a in range(nacc):
                    for kc in range(KT_):
                        kk = kh * KT_ + kc
                        st, sp_ = (kk == 0), (kk == KC - 1)
                        if not tm:
                            ins = nc.tensor.matmul(s.ps[bidx[a]][:, 0:T], lhsT=wt[:, kc, a * 128:(a + 1) * 128],
                                                   rhs=act(kk), start=st, stop=sp_)
                        else:
                            ins = nc.tensor.matmul(s.ps[bidx[a]][:, 0:CW], lhsT=act(kk)[:, a * 128:(a + 1) * 128],
                                                   rhs=wt[:, kc, :], start=st, stop=sp_)
                last = kh == nkh - 1
                cx.release("pe", ins, reads=[wbuf] + abufs, writes=pb if last else [])
            for a in range(nacc):
                if not tm:
                    epi((col // 128) + a, s.ps[bidx[a]][:, 0:T], pb[a])
                else:
                    epi(a, col, CW, s.ps[bidx[a]][:, 0:CW], pb[a])
            col += CW
            s.pump()

    def norm(s, src, srcb, T, gi, dst, dstb, sq_scratch=None):
        if s.planning:
            return
        c, cx, nc, t, b = s.c, s.cx, s.nc, s.t, s.b
        DC = c.DC
        for kc in range(DC):
            cx.op("act", lambda e: e.activation(out=dst[:, kc, 0:T], in_=src[:, kc, 0:T], func=AF.Square),
                  reads=[srcb], writes=[dstb])
        bi = s.banks(1)[0]
        cx.acquire("pe", reads=[dstb, b["const"]], writes=[s.psb[bi]])
        for kc in range(DC):
            ins = nc.tensor.matmul(s.ps[bi][:, 0:T], lhsT=t["ones"][:], rhs=dst[:, kc, 0:T],
                                   start=(kc == 0), stop=(kc == DC - 1))
        cx.release("pe", ins, reads=[dstb, b["const"]], writes=[s.psb[bi]])
        r = t["rstd"]
        cx.op("dve", lambda e: e.tensor_scalar(out=r[:, 0:T], in0=s.ps[bi][:, 0:T], scalar1=1.0 / c.D, scalar2=EPS,
                                               op0=ALU.mult, op1=ALU.add), reads=[s.psb[bi]], writes=[b["rstd"]])
        cx.op("act", lambda e: e.activation(out=r[:, 0:T], in_=r[:, 0:T], func=AF.Sqrt), reads=[b["rstd"]], writes=[b["rstd"]])
        cx.op("dve", lambda e: e.reciprocal(out=r[:, 0:T], in_=r[:, 0:T]), reads=[b["rstd"]], writes=[b["rstd"]])
        for kc in range(DC):
            cx.op("dve", lambda e: e.scalar_tensor_tensor(
                out=dst[:, kc, 0:T], in0=src[:, kc, 0:T], scalar=t["gcol"][:, gi * DC + kc:gi * DC + kc + 1],
                in1=r[:, 0:T], op0=ALU.mult, op1=ALU.mult), reads=[srcb, b["rstd"], b["const"]], writes=[dstb])

    def load_x(s, col0, T):
        if s.planning:
            return
        c, cx, t, b = s.c, s.cx, s.t, s.b
        src = s.d["xT"].rearrange("(k p) t -> p k t", p=128)
        step = 8
        for k in range(0, c.DC, step):
            cx.dma("sp", t["xblk"][:, k:k + step, 0:T], src[:, k:k + step, col0:col0 + T], writes=[b["xblk"]])

    def mem_kv(s):
        c, cx, t, b, d = s.c, s.cx, s.t, s.b, s.d
        M = c.M
        if s.load_mem:
            if not s.planning:
                cx.dma("sp", t["KT"][:].rearrange("p f m -> p (f m)"), d["KT_i"], writes=[b["KT"]])
                cx.dma("sp", t["V"][:].rearrange("p f m -> p (f m)"), d["V_i"], writes=[b["V"]])
            return
        if not s.planning:
            src = d["memT"].rearrange("(k p) t -> p k t", p=128)
            for k in range(0, c.DC, 8):
                cx.dma("sp", t["xblk"][:, k:k + 8, 0:M], src[:, k:k + 8, :], writes=[b["xblk"]])
        s.norm(t["xblk"], b["xblk"], M, 2, t["hT"], b["hT"])
        act = lambda kk: t["hT"][:, kk, 0:M]

        def epiK(f, ps, pbuf):
            cx.op("act", lambda e: e.activation(out=t["KT"][:, f, :], in_=ps, func=AF.Copy),
                  reads=[pbuf], writes=[b["KT"]])
        s.gemm(d["w_mem_kv"], 0, c.DC, 0, c.MEMW, act, M, epiK, [b["hT"]])

        def epiV(tc, cw0, CW, ps, pbuf):
            cx.op("act", lambda e: e.activation(out=t["V"][:, tc, cw0:cw0 + CW], in_=ps, func=AF.Copy),
                  reads=[pbuf], writes=[b["V"]])
        s.gemm(d["w_mem_kv"], 0, c.DC, c.MEMW, c.MEMW, act, M, epiV, [b["hT"]], tm=True)
        if not s.planning:
            cx.dma("sp", d["KT_o"], t["KT"][:].rearrange("p f m -> p (f m)"), reads=[b["KT"]])
            cx.dma("sp", d["V_o"], t["V"][:].rearrange("p f m -> p (f m)"), reads=[b["V"]])

    def mem_attn(s, w_in, mq_c0, interleave=False):
        c, cx, nc, t, b = s.c, s.cx, s.nc, s.t, s.b
        TB = c.TB
        act = lambda kk: t["hT"][:, kk, 0:TB]

        def epi(f, ps, pbuf):
            cx.op("act", lambda e: e.activation(out=t["mqT"][:, f, :], in_=ps, func=AF.Copy),
                  reads=[pbuf], writes=[b["mqT"]])
        s.gemm(w_in, 0, c.DC, mq_c0, c.MEMW, act, TB, epi, [b["hT"]])
        if s.planning:
            return
        if interleave:
            s.gen = s.mem_steps()
        else:
            for _ in s.mem_steps():
                pass

    def pump(s):
        g = getattr(s, "gen", None)
        if g is not None:
            try:
                next(g)
            except StopIteration:
                s.gen = None

    def mem_steps(s):
        c, cx, nc, t, b = s.c, s.cx, s.nc, s.t, s.b
        TB = c.TB
        scl = 256.0 ** -0.5
        it = 0
        for mh in range(c.MEMH):
            for tc in range(c.NTC):
                par = it % 2
                it += 1
                sm = t["small"][:, par * 4:par * 4 + 4]
                smb, peb, pnb = b["small"][par], b["pexp"][par], b["pn"][par]
                pexp, pn = t["pexp"][:, par, :], t["pn"][:, par, :]
                bi = s.banks(1)[0]
                cx.acquire("pe", reads=[b["mqT"], b["KT"]], writes=[s.psb[bi]])
                for j in range(2):
                    ins = nc.tensor.matmul(s.ps[bi][:, 0:c.M], lhsT=t["mqT"][:, mh * 2 + j, tc * 128:(tc + 1) * 128],
                                           rhs=t["KT"][:, mh * 2 + j, :], start=(j == 0), stop=(j == 1))
                cx.release("pe", ins, reads=[b["mqT"], b["KT"]], writes=[s.psb[bi]])
                yield
                psr = s.ps[bi][:, 0:c.M]
                cx.op("dve", lambda e: e.reduce_max(out=sm[:, 0:1], in_=psr, axis=AX.X),
                      reads=[s.psb[bi]], writes=[smb])
                cx.op("dve", lambda e: e.tensor_scalar(out=sm[:, 1:2], in0=sm[:, 0:1], scalar1=-scl, scalar2=None,
                                                       op0=ALU.mult), reads=[smb], writes=[smb])
                cx.op("dve", lambda e: e.memset(sm[:, 2:3], 0.0), writes=[smb])
                cx.op("act", lambda e: e.activation(out=pexp, in_=psr, func=AF.Exp, bias=sm[:, 1:2],
                                                    scale=scl, accum_out=sm[:, 2:3]),
                      reads=[s.psb[bi], smb], writes=[peb, smb])
                cx.op("dve", lambda e: e.reciprocal(out=sm[:, 3:4], in_=sm[:, 2:3]),
                      reads=[smb], writes=[smb])
                cx.op("dve", lambda e: e.tensor_scalar(out=pn, in0=pexp, scalar1=sm[:, 3:4],
                                                       scalar2=None, op0=ALU.mult),
                      reads=[peb, smb], writes=[pnb])
                for mc in range(c.MC):
                    b2 = s.banks(1)[0]
                    cx.acquire("pe", reads=[pnb, b["const"]], writes=[s.psb[b2]])
                    ins = nc.tensor.transpose(s.ps[b2][:, 0:128], pn[:, mc * 128:(mc + 1) * 128], t["ident"][:])
                    cx.release("pe", ins, reads=[pnb, b["const"]], writes=[s.psb[b2]])
                    cx.op("act", lambda e: e.activation(out=t["pT"][:, mc, tc * 128:(tc + 1) * 128],
                                                        in_=s.ps[b2][:, 0:128], func=AF.Copy),
                          reads=[s.psb[b2]], writes=[b["pT"]])
                yield
            for j in range(2):
                bi = s.banks(1)[0]
                cx.acquire("pe", reads=[b["pT"], b["V"]], writes=[s.psb[bi]])
                for mc in range(c.MC):
                    ins = nc.tensor.matmul(s.ps[bi][:, 0:TB], lhsT=t["V"][:, mc, mh * 256 + j * 128:mh * 256 + (j + 1) * 128],
                                           rhs=t["pT"][:, mc, 0:TB], start=(mc == 0), stop=(mc == c.MC - 1))
                cx.release("pe", ins, reads=[b["pT"], b["V"]], writes=[s.psb[bi]])
                cx.op("act", lambda e: e.activation(out=t["catT"][:, c.TOKC + mh * 2 + j, :], in_=s.ps[bi][:, 0:TB],
                                                    func=AF.Copy), reads=[s.psb[bi]], writes=[b["catT"]])

    def pool_mixer(s, blk):
        c, cx, nc, t, b, d = s.c, s.cx, s.nc, s.t, s.b, s.d
        TB, GC = c.TB, c.GC
        W = HALO + TB
        act = lambda kk: t["hT"][:, kk, 0:TB]
        wins = (2, 4, 8, 16)

        def inproj(g):
            ug, ugb = t["ugrp"][g % 2], b["ugrp"][g % 2]
            if not s.planning:
                cx.op("dve", lambda e: e.tensor_copy(out=ug[:, :, 0:HALO], in_=t["uhalo"][:, g * GC:(g + 1) * GC, :]),
                      reads=[b["uhalo"]], writes=[ugb])

            def epi(f, ps, pbuf):
                cx.op("act", lambda e: e.activation(out=ug[:, f, HALO:W], in_=ps, func=AF.Copy),
                      reads=[pbuf], writes=[ugb])
            s.gemm(d["w_in"], 0, c.DC, g * c.G, c.G, act, TB, epi, [b["hT"]])
            if not s.planning:
                cx.op("dve", lambda e: e.tensor_copy(out=t["uhalo"][:, g * GC:(g + 1) * GC, :], in_=ug[:, :, TB:W]),
                      reads=[ugb], writes=[b["uhalo"]])

        def pooling(g):
            w = wins[g]
            ug, ugb = t["ugrp"][g % 2], b["ugrp"][g % 2]
            nst = {2: 1, 4: 2, 8: 3, 16: 4}[w]
            for fc in range(GC):
                u = ug[:, fc, :]
                cur, curb = u, ugb
                for st in range(nst):
                    sh = 1 << st
                    dst, dstb = (t["sa"], b["sa"]) if st % 2 == 0 else (t["sbb"], b["sbb"])
                    cx.op("dve", lambda e: e.tensor_tensor(out=dst[:, sh:W], in0=cur[:, sh:W], in1=cur[:, 0:W - sh],
                                                           op=ALU.add), reads=[curb], writes=[dstb])
                    cur, curb = dst[:, :], dstb
                lo = HALO if blk == 0 else 0
                cx.op("dve", lambda e: e.scalar_tensor_tensor(
                    out=t["pgrp"][:, fc, lo:TB], in0=cur[:, HALO + lo:W], scalar=1.0 / w, in1=u[:, HALO + lo:W],
                    op0=ALU.mult, op1=ALU.subtract), reads=[curb, ugb], writes=[b["pgrp"]])
                if blk == 0:
                    cx.op("dve", lambda e: e.tensor_tensor(out=t["t16"][:], in0=cur[:, HALO:2 * HALO],
                                                           in1=t["invc"][:, g, :], op=ALU.mult),
                          reads=[curb, b["const"]], writes=[b["t16"]])
                    cx.op("dve", lambda e: e.tensor_tensor(out=t["pgrp"][:, fc, 0:HALO], in0=t["t16"][:],
                                                           in1=u[:, HALO:2 * HALO], op=ALU.subtract),
                          reads=[b["t16"], ugb], writes=[b["pgrp"]])

        def grp(g):
            def epi2(f, ps, pbuf):
                cx.op("act", lambda e: e.activation(out=t["catT"][:, g * GC + f, :], in_=ps, func=AF.Copy,
                                                    scale=t["pscale"][:, g * GC + f:g * GC + f + 1]),
                      reads=[pbuf, b["const"]], writes=[b["catT"]])
            s.gemm(d["w_grp"], g * GC, GC, 0, c.G, lambda kk: t["pgrp"][:, kk, :], TB, epi2, [b["pgrp"]])

        inproj(0)
        for g in range(4):
            if g + 1 < 4:
                inproj(g + 1)
            if not s.planning:
                pooling(g)
            grp(g)

    def rope(s, src, srcb, dst, dstb, col0):
        c, cx, t, b = s.c, s.cx, s.t, s.b
        TB = c.TB
        cs = t["cos"][:, 0:TB]
        sn = t["sin"][:, 0:TB]
        ra, rb = t["ra"], t["rb"]
        cb = b["cos"]
        cx.op("dve", lambda e: e.tensor_tensor(out=ra[:], in0=src[:, 0, :], in1=cs, op=ALU.mult), reads=[srcb, cb], writes=[b["ra"]])
        cx.op("dve", lambda e: e.tensor_tensor(out=rb[:], in0=src[:, 1, :], in1=sn, op=ALU.mult), reads=[srcb, b["sin"]], writes=[b["rb"]])
        cx.op("dve", lambda e: e.tensor_tensor(out=dst[:, 0, :], in0=ra[:], in1=rb[:], op=ALU.subtract),
              reads=[b["ra"], b["rb"]], writes=[dstb])
        cx.op("dve", lambda e: e.tensor_tensor(out=ra[:], in0=src[:, 0, :], in1=sn, op=ALU.mult), reads=[srcb, b["sin"]], writes=[b["ra"]])
        cx.op("dve", lambda e: e.tensor_tensor(out=rb[:], in0=src[:, 1, :], in1=cs, op=ALU.mult), reads=[srcb, cb], writes=[b["rb"]])
        cx.op("dve", lambda e: e.tensor_tensor(out=dst[:, 1, :], in0=ra[:], in1=rb[:], op=ALU.add),
              reads=[b["ra"], b["rb"]], writes=[dstb])

    def nm(s, name, hb):
        return name if hb == 0 else name + "1"

    def ret_head_gemms(s, w_in, h):
        c, cx, nc, t, b = s.c, s.cx, s.nc, s.t, s.b
        TB = c.TB
        hb = h % 2
        kT, kTb = t[s.nm("kT", hb)], b[s.nm("kT", hb)]
        vtm, vtmb = t[s.nm("vtm", hb)], b[s.nm("vtm", hb)]
        act = lambda kk: t["hT"][:, kk, 0:TB]

        def epik(f, ps, pbuf):
            cx.op("act", lambda e: e.activation(out=kT[:, f, :], in_=ps, func=AF.Copy), reads=[pbuf], writes=[kTb])
        s.gemm(w_in, 0, c.DC, c.TOK + h * 256, 256, act, TB, epik, [b["hT"]])

        def epiv(tc, cw0, CW, ps, pbuf):
            cx.op("act", lambda e: e.activation(out=vtm[:, tc, :], in_=ps, func=AF.Copy), reads=[pbuf], writes=[vtmb])
        s.gemm(w_in, 0, c.DC, 2 * c.TOK + h * 256, 256, act, TB, epiv, [b["hT"]], tm=True)

    def ret_head_post(s, h, blk):
        c, cx, nc, t, b, d = s.c, s.cx, s.nc, s.t, s.b, s.d
        hb = h % 2
        kr, krB = t[s.nm("kr", hb)], b[s.nm("kr", hb)]
        kz, kzB = t[s.nm("kz", hb)], b[s.nm("kz", hb)]
        vtm, vtmB = t[s.nm("vtm", hb)], b[s.nm("vtm", hb)]
        krb, krbB = t[s.nm("krb", hb)], b[s.nm("krb", hb)]
        for tc in range(c.NTC):
            bi = s.banks(1)[0]
            cx.acquire("pe", reads=[krB, b["const"]], writes=[s.psb[bi]])
            for j in range(2):
                ins = nc.tensor.transpose(s.ps[bi][:, j * 128:(j + 1) * 128], kr[:, j, tc * 128:(tc + 1) * 128], t["ident"][:])
            cx.release("pe", ins, reads=[krB, b["const"]], writes=[s.psb[bi]])
            cx.op("act", lambda e: e.activation(out=kz[:, tc, :], in_=s.ps[bi][:, 0:256], func=AF.Copy,
                                                scale=t["zeta16"][:, h:h + 1]), reads=[s.psb[bi], b["const"]], writes=[kzB])
        for j in range(2):
            cx.op("act", lambda e: e.activation(out=krb[:, j, :], in_=kr[:, j, :], func=AF.Copy), reads=[krB], writes=[krbB])
        r0 = (h * c.NBLK + blk) * 128
        cx.dma("sp", d["kz_o"][r0:r0 + 128, :], kz[:].rearrange("p n e -> p (n e)"), reads=[kzB])
        cx.dma("sp", d["v_o"][r0:r0 + 128, :], vtm[:].rearrange("p n e -> p (n e)"), reads=[vtmB])
        cx.dma("sp", d["kr_o"][r0:r0 + 128, :], krb[:].rearrange("p j t -> p (j t)"), reads=[krbB])
        for n in range(c.NTC):
            s.state_update(h, n, kz, kzB, vtm, vtmB)

    def state_update(s, h, n, kz=None, kzB=None, vtm=None, vtmB=None):
        c, cx, nc, t, b = s.c, s.cx, s.nc, s.t, s.b
        if kz is None:
            kz, kzB, vtm, vtmB = t["kz"], b["kz"], t["vtm"], b["vtm"]
        for j in range(2):
            bi = s.banks(1)[0]
            cx.acquire("pe", reads=[kzB, vtmB], writes=[s.psb[bi]])
            ins = nc.tensor.matmul(s.ps[bi][:, 0:256], lhsT=kz[:, n, j * 128:(j + 1) * 128], rhs=vtm[:, n, :],
                                   start=True, stop=True)
            cx.release("pe", ins, reads=[kzB, vtmB], writes=[s.psb[bi]])
            st = t["state"][:, h, j, :]
            cx.op("dve", lambda e: e.scalar_tensor_tensor(out=st, in0=st, scalar=s.cds[h], in1=s.ps[bi][:, 0:256],
                                                          op0=ALU.mult, op1=ALU.add),
                  reads=[s.psb[bi], b["state"][h]], writes=[b["state"][h]])

    def ret_pass1(s, blk):
        c, t, b = s.c, s.t, s.b
        w = s.d["w_in2"]
        s.ret_head_gemms(w, 0)
        for h in range(c.RH):
            hb = h % 2
            if not s.planning:
                s.rope(t[s.nm("kT", hb)], b[s.nm("kT", hb)], t[s.nm("kr", hb)], b[s.nm("kr", hb)], blk * c.TB)
            if h + 1 < c.RH:
                s.ret_head_gemms(w, h + 1)
            if s.planning:
                continue
            s.ret_head_post(h, blk)

    def head_pre(s, h, blk):
        c, cx, nc, t, b, d = s.c, s.cx, s.nc, s.t, s.b, s.d
        TB = c.TB
        r0 = (h * c.NBLK + blk) * 128
        cx.dma("sp", t["kz"][:].rearrange("p n e -> p (n e)"), d["kz_i"][r0:r0 + 128, :], writes=[b["kz"]])
        cx.dma("sp", t["vtm"][:].rearrange("p n e -> p (n e)"), d["v_i"][r0:r0 + 128, :], writes=[b["vtm"]])
        cx.dma("sp", t["krb"][:].rearrange("p j t -> p (j t)"), d["kr_i"][r0:r0 + 128, :], writes=[b["krb"]])
        s.rope(t["qT"], b["qT"], t["qr"], b["qr"], blk * TB)
        for j in range(2):
            for n in range(c.NTC):
                cx.op("dve", lambda e: e.tensor_tensor(out=t["qxi"][:, j, n * 128:(n + 1) * 128],
                                                       in0=t["qr"][:, j, n * 128:(n + 1) * 128],
                                                       in1=t["xi"][:, h, :], op=ALU.mult),
                      reads=[b["qr"], b["xi"]], writes=[b["qxi"]])

    def ret_mixer(s, blk):
        c, cx, nc, t, b, d = s.c, s.cx, s.nc, s.t, s.b, s.d
        TB = c.TB
        act = lambda kk: t["hT"][:, kk, 0:TB]
        def head_gemms(h):
            def epiq(f, ps, pbuf):
                cx.op("act", lambda e: e.activation(out=t["qT"][:, f, :], in_=ps, func=AF.Copy), reads=[pbuf], writes=[b["qT"]])
            s.gemm(d["w_in"], 0, c.DC, h * 256, 256, act, TB, epiq, [b["hT"]])

            def epig(f, ps, pbuf):
                cx.op("act", lambda e: e.activation(out=t["sg"][:, h % 2, f, :], in_=ps, func=AF.Silu), reads=[pbuf], writes=[b["sg"][h % 2]])
            s.gemm(d["w_in"], 0, c.DC, 3 * c.TOK + h * 256, 256, act, TB, epig, [b["hT"]])

        head_gemms(0)
        for h in range(c.RH):
            if not s.planning:
                s.head_pre(h, blk)
            if h + 1 < c.RH:
                head_gemms(h + 1)
            if s.planning:
                continue
            for n in range(c.NTC):
                sl = slice(n * 128, (n + 1) * 128)
                bi = s.banks(1)[0]
                cx.acquire("pe", reads=[b["krb"], b["qr"]], writes=[s.psb[bi]])
                for j in range(2):
                    ins = nc.tensor.matmul(s.ps[bi][:, 0:128], lhsT=t["krb"][:, j, sl], rhs=t["qr"][:, j, sl],
                                           start=(j == 0), stop=(j == 1))
                cx.release("pe", ins, reads=[b["krb"], b["qr"]], writes=[s.psb[bi]])
                cx.op("dve", lambda e: e.scalar_tensor_tensor(out=t["sT"][:, n, :], in0=s.ps[bi][:, 0:128], scalar=1.0 / 16.0,
                                                              in1=t["maskT"][:, h, :], op0=ALU.mult, op1=ALU.mult),
                      reads=[s.psb[bi], b["maskT"]], writes=[b["sT"][n]])
                cx.op("act", lambda e: e.activation(out=t["stb"][:], in_=t["state"][:, h, :, :], func=AF.Copy),
                      reads=[b["state"][h]], writes=[b["stb"]])
                bo = s.banks(1)[0]
                rd = [b["vtm"], b["sT"][n], b["stb"], b["qxi"]]
                cx.acquire("pe", reads=rd, writes=[s.psb[bo]])
                for e_ in range(2):
                    es = slice(e_ * 128, (e_ + 1) * 128)
                    nc.tensor.matmul(s.ps[bo][:, es], lhsT=t["vtm"][:, n, es], rhs=t["sT"][:, n, :], start=True, stop=False)
                    for j in range(2):
                        ins = nc.tensor.matmul(s.ps[bo][:, es], lhsT=t["stb"][:, j, es], rhs=t["qxi"][:, j, sl],
                                               start=False, stop=(j == 1))
                cx.release("pe", ins, reads=rd, writes=[s.psb[bo]])
                for e_ in range(2):
                    cx.op("act", lambda e: e.activation(out=t["o"][:, e_, sl], in_=s.ps[bo][:, e_ * 128:(e_ + 1) * 128],
                                                        func=AF.Copy), reads=[s.psb[bo]], writes=[b["o"][n]])
                s.state_update(h, n)
            for e_ in range(2):
                cx.op("act", lambda e: e.activation(out=t["osq"][:, e_, :], in_=t["o"][:, e_, :], func=AF.Square),
                      reads=b["o"], writes=[b["osq"]])
            bi = s.banks(1)[0]
            cx.acquire("pe", reads=[b["osq"], b["const"]], writes=[s.psb[bi]])
            for e_ in range(2):
                ins = nc.tensor.matmul(s.ps[bi][:, 0:TB], lhsT=t["ones"][:], rhs=t["osq"][:, e_, :], start=(e_ == 0), stop=(e_ == 1))
            cx.release("pe", ins, reads=[b["osq"], b["const"]], writes=[s.psb[bi]])
            r = t["rstd"]
            cx.op("dve", lambda e: e.tensor_scalar(out=r[:], in0=s.ps[bi][:, 0:TB], scalar1=1.0 / 256.0, scalar2=EPS,
                                                   op0=ALU.mult, op1=ALU.add), reads=[s.psb[bi]], writes=[b["rstd"]])
            cx.op("act", lambda e: e.activation(out=r[:], in_=r[:], func=AF.Sqrt), reads=[b["rstd"]], writes=[b["rstd"]])
            cx.op("dve", lambda e: e.reciprocal(out=r[:], in_=r[:]), reads=[b["rstd"]], writes=[b["rstd"]])
            for e_ in range(2):
                cx.op("dve", lambda e: e.tensor_tensor(out=t["ra"][:], in0=t["o"][:, e_, :], in1=r[:], op=ALU.mult),
                      reads=b["o"] + [b["rstd"]], writes=[b["ra"]])
                cx.op("dve", lambda e: e.tensor_tensor(out=t["catT"][:, h * 2 + e_, :], in0=t["ra"][:], in1=t["sg"][:, h % 2, e_, :],
                                                       op=ALU.mult), reads=[b["ra"], b["sg"][h % 2]], writes=[b["catT"]])

    def out_and_ffn(s, col0, gi_ffn):
        c, cx, nc, t, b, d = s.c, s.cx, s.nc, s.t, s.b, s.d
        TB = c.TB
        if not s.planning:
            s.load_x(col0, TB)

        def epi_res(f, ps, pbuf):
            cx.op("dve", lambda e: e.tensor_tensor(out=t["xblk"][:, f, :], in0=t["xblk"][:, f, :], in1=ps, op=ALU.add),
                  reads=[pbuf, b["xblk"]], writes=[b["xblk"]])
        s.gemm(d["w_out"], 0, c.DC, 0, c.D, lambda kk: t["catT"][:, kk, 0:TB], TB, epi_res, [b["catT"]])
        s.norm(t["xblk"], b["xblk"], TB, gi_ffn, t["hT"], b["hT"])
        aT, aTb = t["catT"], b["catT"]
        for jb in range(c.NFB):
            def epi_up(f, ps, pbuf):
                cx.op("act", lambda e: e.activation(out=t["relu"][:], in_=ps, func=AF.Relu), reads=[pbuf], writes=[b["relu"]])
                cx.op("act", lambda e: e.activation(out=aT[:, f, :], in_=t["relu"][:], func=AF.Square),
                      reads=[b["relu"]], writes=[aTb])
            s.gemm(d["w_up"], 0, c.DC, jb * c.FB * 128, c.FB * 128, lambda kk: t["hT"][:, kk, 0:TB], TB, epi_up, [b["hT"]])
            s.gemm(d["w_down"], jb * c.FB, c.FB, 0, c.D, lambda kk: aT[:, kk, 0:TB], TB, epi_res, [aTb])

    def store_x(s, blk, final_gi=None):
        if s.planning:
            return
        c, cx, t, b, d = s.c, s.cx, s.t, s.b, s.d
        TB = c.TB
        dst = d["xT_out"].rearrange("(k p) t -> p k t", p=128)
        for k in range(0, c.DC, 8):
            cx.dma("sp", dst[:, k:k + 8, blk * TB:(blk + 1) * TB], t["xblk"][:, k:k + 8, :], reads=[b["xblk"]])
        if final_gi is not None:
            s.norm_stats(t["xblk"], b["xblk"], TB)
            dstn = d["xnT_out"].rearrange("(k p) t -> p k t", p=128)
            hc = c.DC // 2
            for half in range(2):
                for kk in range(hc):
                    kc = half * hc + kk
                    cx.op("dve", lambda e: e.scalar_tensor_tensor(
                        out=t["xnv"][:, kk, :], in0=t["xblk"][:, kc, :], scalar=t["gcol"][:, final_gi * c.DC + kc:final_gi * c.DC + kc + 1],
                        in1=t["rstd"][:], op0=ALU.mult, op1=ALU.mult), reads=[b["xblk"], b["rstd"], b["const"]], writes=[b["hT"]])
                cx.dma("sp", dstn[:, half * hc:(half + 1) * hc, blk * TB:(blk + 1) * TB], t["xnv"][:], reads=[b["hT"]])

    def norm_stats(s, src, srcb, T):
        c, cx, nc, t, b = s.c, s.cx, s.nc, s.t, s.b
        dst, dstb = t["hT"], b["hT"]
        for kc in range(c.DC):
            cx.op("act", lambda e: e.activation(out=dst[:, kc, 0:T], in_=src[:, kc, 0:T], func=AF.Square),
                  reads=[srcb], writes=[dstb])
        bi = s.banks(1)[0]
        cx.acquire("pe", reads=[dstb, b["const"]], writes=[s.psb[bi]])
        for kc in range(c.DC):
            ins = nc.tensor.matmul(s.ps[bi][:, 0:T], lhsT=t["ones"][:], rhs=dst[:, kc, 0:T], start=(kc == 0), stop=(kc == c.DC - 1))
        cx.release("pe", ins, reads=[dstb, b["const"]], writes=[s.psb[bi]])
        r = t["rstd"]
        cx.op("dve", lambda e: e.tensor_scalar(out=r[:, 0:T], in0=s.ps[bi][:, 0:T], scalar1=1.0 / c.D, scalar2=EPS,
                                               op0=ALU.mult, op1=ALU.add), reads=[s.psb[bi]], writes=[b["rstd"]])
        cx.op("act", lambda e: e.activation(out=r[:, 0:T], in_=r[:, 0:T], func=AF.Sqrt), reads=[b["rstd"]], writes=[b["rstd"]])
        cx.op("dve", lambda e: e.reciprocal(out=r[:, 0:T], in_=r[:, 0:T]), reads=[b["rstd"]], writes=[b["rstd"]])

    def body(s):
        c, cx, nc, t, b, d = s.c, s.cx, s.nc, s.t, s.b, s.d
        A = s.kind == "A"
        s.mem_kv()
        if A:
            s.load_x(0, HALO)
            s.norm(t["xblk"], b["xblk"], HALO, 0, t["hT"], b["hT"])

            def epih(f, ps, pbuf):
                cx.op("act", lambda e: e.activation(out=t["uhalo"][:, f, :], in_=ps, func=AF.Copy), reads=[pbuf], writes=[b["uhalo"]])
            s.gemm(d["w_in"], 0, c.DC, 0, c.TOK, lambda kk: t["hT"][:, kk, 0:HALO], HALO, epih, [b["hT"]])
            for blk in range(c.NBLK):
                col0 = HALO + blk * c.TB
                s.load_x(col0, c.TB)
                s.norm(t["xblk"], b["xblk"], c.TB, 0, t["hT"], b["hT"])
                cx.inherit(s.tmpbufs, [b["xblk"]])
                s.mem_attn(d["w_in"], c.TOK, interleave=True)
                s.pool_mixer(blk)
                while getattr(s, "gen", None) is not None:
                    s.pump()
                cx.inherit([b["xblk"]], s.tmpbufs)
                s.out_and_ffn(col0, 1)
                s.store_x(blk)
            if not s.planning:
                for h in range(c.RH):
                    cx.op("dve", lambda e: e.memset(t["state"][:, h, :, :], 0.0), writes=[b["state"][h]])
            xo = d["xT_out"].rearrange("(k p) t -> p k t", p=128)
            for blk in range(c.NBLK):
                if not s.planning:
                    for key in cx.dslots["sp"]:
                        if cx.cnt[key] > 0:
                            cx._need("sp", (key, cx.cnt[key]))
                    for k in range(0, c.DC, 8):
                        cx.dma("sp", t["xblk"][:, k:k + 8, :], xo[:, k:k + 8, blk * c.TB:(blk + 1) * c.TB], writes=[b["xblk"]])
                s.norm(t["xblk"], b["xblk"], c.TB, 3, t["hT"], b["hT"])
                cx.inherit(s.tmpbufs, [b["xblk"]])
                s.rope_tables(blk, False)
                s.ret_pass1(blk)
                cx.inherit([b["xblk"]], s.tmpbufs)
            if not s.planning:
                lo = d["Lout"].rearrange("(h j p) e -> p h j e", p=128, j=2)
                cx.dma("sp", lo, t["state"][:], reads=b["state"])
        else:
            if not s.planning:
                cx.inherit(s.LstB, [b["catT"]])
                la = d["Lall"].rearrange("(r h j p) e -> p r h j e", p=128, j=2, h=c.RH)
                for h in range(c.RH):
                    Lt, Lb = s.LstT[h % 2], s.LstB[h % 2]
                    for j in range(2):
                        cx.dma("sp", Lt[:, :, j, :], la[:, :, h, j, :], writes=[Lb])
                    st = t["state"][:, h, :, :]
                    for r in range(c.NCORES):
                        wcol = t["wS"][:, r * c.RH + h:r * c.RH + h + 1]
                        if r == 0:
                            cx.op("dve", lambda e: e.tensor_scalar(out=st, in0=Lt[:, 0, :, :], scalar1=wcol, scalar2=None,
                                                                   op0=ALU.mult), reads=[Lb, b["const"]], writes=[b["state"][h]])
                        else:
                            cx.op("dve", lambda e: e.scalar_tensor_tensor(out=st, in0=Lt[:, r, :, :], scalar=wcol, in1=st,
                                                                          op0=ALU.mult, op1=ALU.add),
                                  reads=[Lb, b["const"], b["state"][h]], writes=[b["state"][h]])
                cx.inherit([b["catT"]], s.LstB)
            cx.inherit([b["xblk"]], s.tmpbufs)
            for blk in range(c.NBLK):
                col0 = HALO + blk * c.TB
                s.load_x(col0, c.TB)
                s.norm(t["xblk"], b["xblk"], c.TB, 0, t["hT"], b["hT"])
                cx.inherit(s.tmpbufs, [b["xblk"]])
                s.rope_tables(blk, True)
                s.ret_mixer(blk)
                cx.inherit(s.membufs, s.tmpbufs)
                s.mem_attn(d["w_in"], 4 * c.TOK)
                cx.inherit([b["xblk"]], s.tmpbufs)
                s.out_and_ffn(col0, 1)
                s.store_x(blk, final_gi=3)


_PROGS = {}


def _ret_consts(cfg):
    H, C = cfg.RH, 128
    log_g = np.log1p(-(np.float32(2.0) ** (-5.0 - np.arange(H, dtype=np.float32)))).astype(np.float32)
    idx = np.arange(C, dtype=np.float32)
    diff = idx[:, None] - idx[None, :]
    decay_in = np.where(diff >= 0, np.exp(log_g[:, None, None] * np.maximum(diff, 0.0)), 0.0).astype(np.float32)
    xi = np.exp(log_g[None, :] * (idx + 1.0)[:, None]).astype(np.float32)
    zeta = np.exp(log_g[None, :] * (C - 1.0 - idx)[:, None]).astype(np.float32)
    cd = np.exp(log_g.astype(np.float64) * C)
    maskT = np.ascontiguousarray(decay_in.transpose(2, 0, 1)).reshape(C, H * C)
    xi_rep = np.ascontiguousarray(np.broadcast_to(xi.T[None, :, :], (128, H, C))).reshape(128, H * C)
    zeta16 = np.ascontiguousarray(zeta / 16.0).astype(np.float32)
    half = 128
    inv32 = (np.float32(10000.0) ** (-np.arange(half, dtype=np.float32) / np.float32(half))).astype(np.float32)
    invf = (inv32.astype(np.float64) / (2 * np.pi)).astype(np.float32).reshape(128, 1)
    return dict(maskT=maskT.astype(np.float32), xi=xi_rep.astype(np.float32), zeta16=zeta16, invf=invf), cd, log_g


def _col(g, DC):
    return np.ascontiguousarray(g.reshape(DC, 128).T)


def run_model(cfg, x, mem, positions, mix_norm_g, w_in_pool, w_pool_grp, pool_scale, w_out_pool,
              w_in_ret, w_out_ret, mem_norm_g, w_mem_kv, ffn_norm_g, w_up, w_down, final_norm_g, depth):
    NCO, TL, D = cfg.NCORES, cfg.TL, cfg.D
    consts, cd, log_g = _ret_consts(cfg)
    cds = [float(v) for v in cd]
    key = (cfg.TL, cfg.TB, cfg.DFF, cfg.NCORES)
    if key not in _PROGS:
        _PROGS[key] = (Prog(cfg, "A", cds, False), Prog(cfg, "A", cds, True), Prog(cfg, "B", cds, True))
    pA0, pA1, pB = _PROGS[key]
    cores = list(range(NCO))
    xT = np.ascontiguousarray(np.asarray(x)[0].T)
    memT = np.ascontiguousarray(np.asarray(mem)[0].T)
    pos = np.asarray(positions)[0].astype(np.int32)
    ident = np.eye(128, dtype=np.float32)
    wS = np.zeros((NCO, NCO, cfg.RH), np.float64)
    for cc in range(NCO):
        for r in range(cc):
            wS[cc, r, :] = cd ** (cfg.NCH * (cc - 1 - r))
    invc = np.zeros((NCO, 4, 16), np.float32)
    for g, w in enumerate((2, 4, 8, 16)):
        for cc in range(NCO):
            tg = cc * TL + np.arange(16)
            invc[cc, g] = 1.0 / np.minimum(tg + 1, w)

    def halo_x(xT_full, cc):
        out = np.zeros((D, HALO + TL), np.float32)
        out[:, HALO:] = xT_full[:, cc * TL:(cc + 1) * TL]
        if cc > 0:
            out[:, :HALO] = xT_full[:, cc * TL - HALO:cc * TL]
        return out

    def common(cc, xT_full, gains):
        m = dict(xT=halo_x(xT_full, cc), ident=ident,
                 gcol=np.ascontiguousarray(np.concatenate([_col(np.asarray(g), cfg.DC) for g in gains], axis=1)),
                 pos=np.ascontiguousarray(np.broadcast_to(pos[None, cc * TL:(cc + 1) * TL], (128, TL))))
        m.update(consts)
        return m

    out = None
    for j in range(depth // 2):
        ip, ir = 2 * j, 2 * j + 1
        ins = []
        for cc in cores:
            m = common(cc, xT, [mix_norm_g[ip], ffn_norm_g[ip], mem_norm_g, mix_norm_g[ir]])
            m.update(w_in=np.asarray(w_in_pool[j]), w_grp=np.asarray(w_pool_grp[j]).reshape(4 * cfg.G, cfg.G),
                     pscale=_col(np.asarray(pool_scale[j]), cfg.TOKC), w_out=np.asarray(w_out_pool[j]),
                     w_up=np.asarray(w_up[ip]), w_down=np.asarray(w_down[ip]), w_in2=np.asarray(w_in_ret[j]),
                     invc=np.ascontiguousarray(np.broadcast_to(invc[cc].reshape(1, 64), (128, 64))))
            if j == 0:
                m.update(memT=memT, w_mem_kv=np.asarray(w_mem_kv))
            else:
                m.update(KT_i=memkv[0], V_i=memkv[1])
            ins.append(m)
        res = run_bass_kernel_spmd((pA0 if j == 0 else pA1).nc, ins, core_ids=cores).results
        if j == 0:
            memkv = (res[0]["KT_o"], res[0]["V_o"])
        xT = np.concatenate([r["xT_out"] for r in res], axis=1)
        Lall = np.concatenate([r["Lout"] for r in res], axis=0)
        kvs = [(r["kz_o"], r["v_o"], r["kr_o"]) for r in res]
        ins = []
        for cc in cores:
            m = common(cc, xT, [mix_norm_g[ir], ffn_norm_g[ir], mem_norm_g, final_norm_g])
            m.update(w_in=np.asarray(w_in_ret[j]), w_out=np.asarray(w_out_ret[j]), w_up=np.asarray(w_up[ir]),
                     w_down=np.asarray(w_down[ir]), Lall=Lall, KT_i=memkv[0], V_i=memkv[1], kz_i=kvs[cc][0], v_i=kvs[cc][1], kr_i=kvs[cc][2],
                     wS=np.ascontiguousarray(np.broadcast_to(wS[cc].reshape(1, -1), (128, NCO * cfg.RH))).astype(np.float32))
            ins.append(m)
        res = run_bass_kernel_spmd(pB.nc, ins, core_ids=cores).results
        xT = np.concatenate([r["xT_out"] for r in res], axis=1)
        out = np.concatenate([r["xnT_out"] for r in res], axis=1)
    return np.ascontiguousarray(out.T)[None].astype(np.float32), np.ascontiguousarray(xT.T)[None]


def kernel(x, mem, positions, mix_norm_g, w_in_pool, w_pool_grp, pool_scale, w_out_pool,
           w_in_ret, w_out_ret, mem_norm_g, w_mem_kv, ffn_norm_g, w_up, w_down, final_norm_g):
    cfg = Cfg()
    out, _ = run_model(cfg, x, mem, positions, mix_norm_g, w_in_pool, w_pool_grp, pool_scale, w_out_pool,
                       w_in_ret, w_out_ret, mem_norm_g, w_mem_kv, ffn_norm_g, w_up, w_down, final_norm_g, 4)
    return out
```

```python
import numpy as np
import concourse.bass as bass
import concourse.mybir as mybir
from concourse.bass_utils import run_bass_kernel_spmd

F32 = mybir.dt.float32
BF16 = mybir.dt.bfloat16
I32 = mybir.dt.int32
ALU = mybir.AluOpType
AF = mybir.ActivationFunctionType
AX = mybir.AxisListType
EPS = 1e-6
HALO = 16
PI = float(np.pi)


class Cfg:
    def __init__(s, D=4096, TL=1024, TB=512, M=256, MEMH=4, DFF=16384, NCORES=8):
        s.D, s.TL, s.TB, s.M, s.MEMH, s.DFF, s.NCORES = D, TL, TB, M, MEMH, DFF, NCORES
        s.DC = D // 128
        s.MEMW = MEMH * 256
        s.MEMC = s.MEMW // 128
        s.TOK = D - s.MEMW
        s.TOKC = s.TOK // 128
        s.G = s.TOK // 4
        s.GC = s.G // 128
        s.RH = s.TOK // 256
        s.NTC = TB // 128
        s.NBLK = TL // TB
        s.MC = M // 128
        s.FB = 16
        s.NFB = DFF // (128 * s.FB)
        s.NCH = TL // 128
        s.INW = 4 * s.TOK + s.MEMW


class Buf:
    def __init__(self, name):
        self.name = name
        self.wr = {}
        self.rd = {}


class Ctx:
    def __init__(self, nc):
        self.nc = nc
        self.eng = {"pe": nc.tensor, "act": nc.scalar, "dve": nc.vector, "pool": nc.gpsimd, "sp": nc.sync}
        self.sem = {k: nc.alloc_semaphore("s_" + k) for k in ("pe", "act", "dve")}
        self.cnt = {k: 0 for k in self.sem}
        self.seen = {k: {} for k in self.eng}
        self.dslots = {}
        for q in ("pool", "sp"):
            self.dslots[q] = []
            for i in range(6):
                key = f"d_{q}{i}"
                self.sem[key] = nc.alloc_semaphore("s_" + key)
                self.cnt[key] = 0
                self.dslots[q].append(key)
        self.drr = {"pool": 0, "sp": 0}
        self.allbufs = []

    def buf(self, name):
        b = Buf(name)
        self.allbufs.append(b)
        return b

    def _need(self, e, dep):
        if dep is None:
            return
        key, c = dep
        if key == e and e == "pe":
            return
        if self.seen[e].get(key, 0) >= c:
            return
        self.seen[e][key] = c
        self.eng[e].wait_ge(self.sem[key], c)

    def acquire(self, e, reads=(), writes=()):
        for b in reads:
            for key, c in b.wr.items():
                self._need(e, (key, c))
        for b in writes:
            for key, c in b.wr.items():
                self._need(e, (key, c))
            for key, c in b.rd.items():
                self._need(e, (key, c))

    def _record(self, key, reads, writes):
        c = self.cnt[key]
        for b in reads:
            if b in writes:
                continue
            b.rd[key] = c
        for b in writes:
            b.wr[key] = c
            b.rd = {}

    def release(self, e, ins, reads=(), writes=()):
        ins.then_inc(self.sem[e], 1)
        self.cnt[e] += 1
        self._record(e, reads, writes)

    def op(self, e, fn, reads=(), writes=()):
        self.acquire(e, reads, writes)
        ins = fn(self.eng[e])
        self.release(e, ins, reads, writes)
        return ins

    def dma(self, q, out, in_, reads=(), writes=()):
        key = self.dslots[q][self.drr[q] % len(self.dslots[q])]
        self.drr[q] += 1
        if self.cnt[key] > 0:
            self._need(q, (key, self.cnt[key]))
        self.acquire(q, reads, writes)
        ins = self.eng[q].dma_start(out=out, in_=in_)
        ins.then_inc(self.sem[key], 16)
        self.cnt[key] += 16
        self._record(key, reads, writes)
        return ins

    def inherit(self, dsts, srcs):
        for d in dsts:
            for s_ in srcs:
                if s_ is d:
                    continue
                for key, c in list(s_.rd.items()) + list(s_.wr.items()):
                    if d.rd.get(key, 0) < c:
                        d.rd[key] = c

    def finish(self):
        for e in ("sp",):
            for key, c in self.cnt.items():
                if c > 0:
                    self.eng[e].wait_ge(self.sem[key], c)


class Prog:
    def __init__(s, cfg, kind, cds, load_mem=False):
        s.c = cfg
        s.kind = kind
        s.load_mem = load_mem
        s.cds = cds
        s.planning = True
        s.tiles = []
        s.build()

    def sb(s, name, shape, dt):
        n = int(np.prod(shape[1:])) * (4 if dt in (F32, I32) else 2)
        off = (s.sboff + 31) // 32 * 32
        s.sboff = off + n
        assert s.sboff <= 229300, (name, s.sboff)
        return s.nc.alloc_sbuf_tensor_at(name, list(shape), dt, offset=off)

    def sb_at(s, name, shape, dt, off):
        return s.nc.alloc_sbuf_tensor_at(name, list(shape), dt, offset=off)

    def dram_in(s, name, shape, dt=F32):
        return s.nc.dram_tensor(name, list(shape), dt, kind="ExternalInput").ap()

    def dram_out(s, name, shape, dt=F32):
        return s.nc.dram_tensor(name, list(shape), dt, kind="ExternalOutput").ap()

    def build(s):
        for planning in (True, False):
            s.planning = planning
            s.nc = bass.Bass("TRN2", target_bir_lowering=False)
            s.cx = Ctx(s.nc)
            s.sboff = 16512
            s.tile_i = 0
            s.tile_issued = 0
            if planning:
                s.tiles = []
            s.setup()
            s.body()
            if not planning:
                s.cx.finish()

    def setup(s):
        c, nc, cx = s.c, s.nc, s.cx
        A = s.kind == "A"
        d = s.d = {}
        d["xT"] = s.dram_in("xT", [c.D, HALO + c.TL])
        if s.load_mem:
            d["KT_i"] = s.dram_in("KT_i", [128, c.MEMC * c.M], BF16)
            d["V_i"] = s.dram_in("V_i", [128, c.MC * c.MEMW], BF16)
        else:
            d["memT"] = s.dram_in("memT", [c.D, c.M])
            d["KT_o"] = s.dram_out("KT_o", [128, c.MEMC * c.M], BF16)
            d["V_o"] = s.dram_out("V_o", [128, c.MC * c.MEMW], BF16)
        d["gcol"] = s.dram_in("gcol", [128, 4 * c.DC])
        d["ident"] = s.dram_in("ident", [128, 128])
        if not s.load_mem:
            d["w_mem_kv"] = s.dram_in("w_mem_kv", [c.D, 2 * c.MEMW])
        d["pos"] = s.dram_in("pos", [128, c.TL], I32)
        d["invf"] = s.dram_in("invf", [128, 1])
        d["zeta16"] = s.dram_in("zeta16", [128, c.RH])
        d["xi"] = s.dram_in("xi", [128, c.RH * 128])
        d["maskT"] = s.dram_in("maskT", [128, c.RH * 128])
        d["w_out"] = s.dram_in("w_out", [c.D, c.D])
        d["w_up"] = s.dram_in("w_up", [c.D, c.DFF])
        d["w_down"] = s.dram_in("w_down", [c.DFF, c.D])
        d["xT_out"] = s.dram_out("xT_out", [c.D, c.TL])
        if A:
            d["w_in"] = s.dram_in("w_in", [c.D, c.D])
            d["w_grp"] = s.dram_in("w_grp", [4 * c.G, c.G])
            d["pscale"] = s.dram_in("pscale", [128, c.TOKC])
            d["invc"] = s.dram_in("invc", [128, 4 * 16])
            d["w_in2"] = s.dram_in("w_in2", [c.D, c.INW])
            d["Lout"] = s.dram_out("Lout", [c.RH * 2 * 128, 256])
            d["kz_o"] = s.dram_out("kz_o", [c.RH * c.NBLK * 128, c.NTC * 256], BF16)
            d["v_o"] = s.dram_out("v_o", [c.RH * c.NBLK * 128, c.NTC * 256], BF16)
            d["kr_o"] = s.dram_out("kr_o", [c.RH * c.NBLK * 128, 2 * c.TB], BF16)
        else:
            d["w_in"] = s.dram_in("w_in", [c.D, c.INW])
            d["Lall"] = s.dram_in("Lall", [c.NCORES * c.RH * 2 * 128, 256])
            d["wS"] = s.dram_in("wS", [128, c.NCORES * c.RH])
            d["kz_i"] = s.dram_in("kz_i", [c.RH * c.NBLK * 128, c.NTC * 256], BF16)
            d["v_i"] = s.dram_in("v_i", [c.RH * c.NBLK * 128, c.NTC * 256], BF16)
            d["kr_i"] = s.dram_in("kr_i", [c.RH * c.NBLK * 128, 2 * c.TB], BF16)
            d["xnT_out"] = s.dram_out("xnT_out", [c.D, c.TL])

        TB, DC = c.TB, c.DC
        t = s.t = {}
        b = s.b = {}
        s.r1 = s.sboff
        t["xblk"] = s.sb("xblk", [128, DC, TB], F32)
        r1 = b["xblk"] = cx.buf("r1")
        s.hT_off = (s.sboff + 31) // 32 * 32
        t["hT"] = s.sb("hT", [128, DC, TB], BF16)
        t["xnv"] = s.sb_at("xnv", [128, DC // 2, TB], F32, s.hT_off)
        b["hT"] = cx.buf("hT")
        s.catT_off = (s.sboff + 31) // 32 * 32
        t["catT"] = s.sb("catT", [128, DC, TB], BF16)
        b["catT"] = cx.buf("catT")
        s.NWB = 4
        t["wb"] = [s.sb(f"wb{i}", [128, 4096], BF16) for i in range(s.NWB)]
        b["wb"] = [cx.buf(f"wb{i}") for i in range(s.NWB)]
        t["rstd"] = s.sb("rstd", [128, TB], F32)
        b["rstd"] = cx.buf("rstd")
        t["ones"] = s.sb("ones", [128, 128], BF16)
        t["ident"] = s.sb("ident", [128, 128], F32)
        t["gcol"] = s.sb("gcol", [128, 4 * DC], F32)
        t["KT"] = s.sb("KT", [128, c.MEMC, c.M], BF16)
        t["V"] = s.sb("V", [128, c.MC, c.MEMW], BF16)
        b["KT"] = cx.buf("KT")
        b["V"] = cx.buf("V")
        b["const"] = cx.buf("const")
        t["state"] = s.sb("state", [128, c.RH, 2, 256], F32)
        b["state"] = [cx.buf(f"state{h}") for h in range(c.RH)]
        t["zeta16"] = s.sb("zeta16", [128, c.RH], F32)
        t["invf"] = s.sb("invf", [128, 1], F32)
        t["small"] = s.sb("small", [128, 16], F32)
        b["small"] = cx.buf("small")
        t["relu"] = s.sb("relu", [128, TB], F32)
        b["relu"] = cx.buf("relu")
        if A:
            t["pscale"] = s.sb("pscale", [128, c.TOKC], F32)
            t["invc"] = s.sb("invc", [128, 4, 16], F32)
            t["uhalo"] = s.sb("uhalo", [128, c.TOKC, 16], F32)
            b["uhalo"] = cx.buf("uhalo")
            t["xh"] = s.sb("xh", [128, DC, HALO], F32)
            b["xh"] = cx.buf("xh")
            t["hTh"] = s.sb("hTh", [128, DC, HALO], BF16)
            b["hTh"] = cx.buf("hTh")
        else:
            t["wS"] = s.sb("wS", [128, c.NCORES * c.RH], F32)
        s.ps = [nc.alloc_psum_tensor(f"ps{i}", [128, 512], F32) for i in range(8)]
        s.psb = [cx.buf(f"ps{i}") for i in range(8)]
        s.bank_rr = 0
        s.grp_rr = 0

        o = s.r1
        s.tmpbufs = []

        def carve(name, shape, dt, at=None):
            nonlocal o
            if at is not None:
                o = at
            n = int(np.prod(shape[1:])) * (4 if dt in (F32, I32) else 2)
            o = (o + 31) // 32 * 32
            h = s.sb_at(name, list(shape), dt, o)
            o += n
            assert o <= s.r1 + DC * TB * 4, (name, o - s.r1)
            t[name] = h
            b[name] = cx.buf(name)
            s.tmpbufs.append(b[name])
            return h
        W = HALO + TB
        carve("cos", [128, TB], F32)
        carve("sin", [128, TB], F32)
        carve("posi", [128, TB], I32)
        carve("posf", [128, TB], F32)
        if not A:
            carve("xi", [128, c.RH, 128], F32)
            carve("maskT", [128, c.RH, 128], F32)
        o_t = o
        carve("mqT", [128, c.MEMC, TB], BF16)
        carve("pexp", [128, 2, c.M], F32)
        carve("pn", [128, 2, c.M], F32)
        carve("pT", [128, c.MC, TB], BF16)
        if A:
            o_pool = o
            carve("ugrp", [128, c.GC, W], F32)
            carve("ugrp1", [128, c.GC, W], F32)
            carve("pgrp", [128, c.GC, TB], BF16)
            carve("sa", [128, W], F32)
            carve("sbb", [128, W], F32)
            carve("t16", [128, 16], F32)
            o = o_pool
            carve("kT", [128, 2, TB], F32)
            carve("kr", [128, 2, TB], F32)
            carve("kT1", [128, 2, TB], F32)
            carve("kr1", [128, 2, TB], F32)
        else:
            o = o_t
        carve("ra", [128, TB], F32)
        carve("rb", [128, TB], F32)
        carve("vtm", [128, c.NTC, 256], BF16)
        carve("kz", [128, c.NTC, 256], BF16)
        carve("krb", [128, 2, TB], BF16)
        if A:
            carve("vtm1", [128, c.NTC, 256], BF16)
            carve("kz1", [128, c.NTC, 256], BF16)
            carve("krb1", [128, 2, TB], BF16)
        if not A:
            carve("qT", [128, 2, TB], F32)
            carve("qr", [128, 2, TB], BF16)
            carve("qxi", [128, 2, TB], BF16)
            carve("sg", [128, 2, 2, TB], F32)
            carve("sT", [128, c.NTC, 128], BF16)
            carve("stb", [128, 2, 256], BF16)
            carve("o", [128, 2, TB], F32)
            carve("osq", [128, 2, TB], BF16)

        s.membufs = [b[n] for n in ("mqT", "pexp", "pn", "pT")]
        b["pexp"] = [b["pexp"], cx.buf("pexp1")]
        b["pn"] = [b["pn"], cx.buf("pn1")]
        b["small"] = [b["small"], cx.buf("small1")]
        s.membufs += [b["pexp"][1], b["pn"][1]]
        s.tmpbufs += [b["pexp"][1], b["pn"][1]]
        if A:
            t["ugrp"] = [t["ugrp"], t["ugrp1"]]
            b["ugrp"] = [b["ugrp"], b["ugrp1"]]
        if not A:
            s.LstT = [s.sb_at(f"Lst{i}", [128, c.NCORES, 2, 256], F32, s.catT_off + i * c.NCORES * 2 * 256 * 4) for i in range(2)]
            s.LstB = [cx.buf("LstA"), cx.buf("LstB")]
            b["sg"] = [b["sg"], cx.buf("sg1")]
            s.tmpbufs.append(b["sg"][1])
            b["sT"] = [b["sT"]] + [cx.buf(f"sT{n}") for n in range(1, c.NTC)]
            b["o"] = [b["o"]] + [cx.buf(f"o{n}") for n in range(1, c.NTC)]
            s.tmpbufs += b["sT"][1:] + b["o"][1:]
        if s.planning:
            return
        cb = b["const"]
        cx.op("dve", lambda e: e.memset(t["ones"][:], 1.0), writes=[cb])
        cx.dma("sp", t["ident"][:], d["ident"], writes=[cb])
        cx.dma("sp", t["gcol"][:], d["gcol"], writes=[cb])
        cx.dma("sp", t["zeta16"][:], d["zeta16"], writes=[cb])
        cx.dma("sp", t["invf"][:], d["invf"], writes=[cb])
        if A:
            cx.dma("sp", t["pscale"][:], d["pscale"], writes=[cb])
            cx.dma("sp", t["invc"][:], d["invc"].rearrange("p (g t) -> p g t", t=16), writes=[cb])
        else:
            cx.dma("sp", t["wS"][:], d["wS"], writes=[cb])

    def rope_tables(s, blk, with_xi):
        if s.planning:
            return
        c, cx, t, b, d = s.c, s.cx, s.t, s.b, s.d
        TB = c.TB
        cb = b["const"]
        cx.dma("sp", t["posi"][:], d["pos"][:, blk * TB:(blk + 1) * TB], writes=[b["posi"]])
        if with_xi:
            cx.dma("sp", t["xi"][:], d["xi"].rearrange("p (h t) -> p h t", t=128), writes=[b["xi"]])
            cx.dma("sp", t["maskT"][:], d["maskT"].rearrange("p (h t) -> p h t", t=128), writes=[b["maskT"]])
        posf, posi, ra, rb = t["posf"], t["posi"], t["ra"], t["rb"]
        bpi, bpf, bra, brb = b["posi"], b["posf"], b["ra"], b["rb"]
        cx.op("dve", lambda e: e.tensor_copy(out=posf[:], in_=posi[:]), reads=[bpi], writes=[bpf])
        for name, off in (("sin", 0.0), ("cos", 0.25)):
            o_, ob = t[name], b[name]
            cx.op("dve", lambda e: e.tensor_scalar(out=o_[:], in0=posf[:], scalar1=t["invf"][:, 0:1], scalar2=off,
                                                   op0=ALU.mult, op1=ALU.add), reads=[bpf, cb], writes=[ob])
            cx.op("dve", lambda e: e.tensor_copy(out=posi[:], in_=o_[:]), reads=[ob], writes=[bpi])
            cx.op("dve", lambda e: e.tensor_copy(out=ra[:], in_=posi[:]), reads=[bpi], writes=[bra])
            cx.op("dve", lambda e: e.tensor_tensor(out=o_[:], in0=o_[:], in1=ra[:], op=ALU.subtract), reads=[ob, bra], writes=[ob])
            cx.op("dve", lambda e: e.tensor_scalar(out=rb[:], in0=o_[:], scalar1=0.5, scalar2=None, op0=ALU.is_gt),
                  reads=[ob], writes=[brb])
            cx.op("dve", lambda e: e.tensor_tensor(out=o_[:], in0=o_[:], in1=rb[:], op=ALU.subtract), reads=[ob, brb], writes=[ob])
            cx.op("dve", lambda e: e.tensor_scalar(out=rb[:], in0=o_[:], scalar1=-0.5, scalar2=None, op0=ALU.is_lt),
                  reads=[ob], writes=[brb])
            cx.op("dve", lambda e: e.tensor_tensor(out=o_[:], in0=o_[:], in1=rb[:], op=ALU.add), reads=[ob, brb], writes=[ob])
            cx.op("act", lambda e: e.activation(out=o_[:], in_=o_[:], func=AF.Sin, scale=6.28318), reads=[ob], writes=[ob])

    def banks(s, n):
        if n == 4:
            g = s.grp_rr % 2
            s.grp_rr += 1
            idx = list(range(4 * g, 4 * g + 4))
        else:
            idx = []
            for _ in range(n):
                idx.append(s.bank_rr % 8)
                s.bank_rr += 1
        return idx

    def issue_tiles(s, upto):
        cx = s.cx
        while s.tile_issued < min(upto, len(s.tiles)):
            i = s.tile_issued
            W, k0, KCt, c0, CW = s.tiles[i]
            slot = i % s.NWB
            dst = s.t["wb"][slot][:, 0:KCt * CW].rearrange("p (k c) -> p k c", c=CW)
            src = W[k0 * 128:(k0 + KCt) * 128, c0:c0 + CW].rearrange("(k p) c -> p k c", p=128)
            cx.dma("pool", dst, src, writes=[s.b["wb"][slot]])
            s.tile_issued += 1

    def gemm(s, W, k0, KC, c0, ncols, act, T, epi, abufs, tm=False):
        cx, nc = s.cx, s.nc
        KT_ = 8 if (KC >= 8 and KC % 8 == 0) else KC
        assert KC % KT_ == 0
        nkh = KC // KT_
        col = 0
        while col < ncols:
            CW = min(512, ncols - col)
            tiles = []
            for kh in range(nkh):
                if s.planning:
                    s.tiles.append((W, k0 + kh * KT_, KT_, c0 + col, CW))
                tiles.append(s.tile_i)
                s.tile_i += 1
            if s.planning:
                col += CW
                continue
            nacc = (CW // 128) if not tm else (T // 128)
            bidx = s.banks(4) if nacc > 2 else s.banks(nacc)
            bidx = bidx[:nacc]
            pb = [s.psb[i] for i in bidx]
            for kh in range(nkh):
                ti = tiles[kh]
                s.issue_tiles(ti + s.NWB)
                slot = ti % s.NWB
                wbuf = s.b["wb"][slot]
                wt = s.t["wb"][slot][:, 0:KT_ * CW].rearrange("p (k c) -> p k c", c=CW)
                cx.acquire("pe", reads=[wbuf] + abufs, writes=pb if kh == 0 else [])
                ins = None
                for a in range(nacc):
                    for kc in range(KT_):
                        kk = kh * KT_ + kc
                        st, sp_ = (kk == 0), (kk == KC - 1)
                        if not tm:
                            ins = nc.tensor.matmul(s.ps[bidx[a]][:, 0:T], lhsT=wt[:, kc, a * 128:(a + 1) * 128],
                                                   rhs=act(kk), start=st, stop=sp_)
                        else:
                            ins = nc.tensor.matmul(s.ps[bidx[a]][:, 0:CW], lhsT=act(kk)[:, a * 128:(a + 1) * 128],
                                                   rhs=wt[:, kc, :], start=st, stop=sp_)
                last = kh == nkh - 1
                cx.release("pe", ins, reads=[wbuf] + abufs, writes=pb if last else [])
            for a in range(nacc):
                if not tm:
                    epi((col // 128) + a, s.ps[bidx[a]][:, 0:T], pb[a])
                else:
                    epi(a, col, CW, s.ps[bidx[a]][:, 0:CW], pb[a])
            col += CW
            s.pump()

    def norm(s, src, srcb, T, gi, dst, dstb, sq_scratch=None):
        if s.planning:
            return
        c, cx, nc, t, b = s.c, s.cx, s.nc, s.t, s.b
        DC = c.DC
        for kc in range(DC):
            cx.op("act", lambda e: e.activation(out=dst[:, kc, 0:T], in_=src[:, kc, 0:T], func=AF.Square),
                  reads=[srcb], writes=[dstb])
        bi = s.banks(1)[0]
        cx.acquire("pe", reads=[dstb, b["const"]], writes=[s.psb[bi]])
        for kc in range(DC):
            ins = nc.tensor.matmul(s.ps[bi][:, 0:T], lhsT=t["ones"][:], rhs=dst[:, kc, 0:T],
                                   start=(kc == 0), stop=(kc == DC - 1))
        cx.release("pe", ins, reads=[dstb, b["const"]], writes=[s.psb[bi]])
        r = t["rstd"]
        cx.op("dve", lambda e: e.tensor_scalar(out=r[:, 0:T], in0=s.ps[bi][:, 0:T], scalar1=1.0 / c.D, scalar2=EPS,
                                               op0=ALU.mult, op1=ALU.add), reads=[s.psb[bi]], writes=[b["rstd"]])
        cx.op("act", lambda e: e.activation(out=r[:, 0:T], in_=r[:, 0:T], func=AF.Sqrt), reads=[b["rstd"]], writes=[b["rstd"]])
        cx.op("dve", lambda e: e.reciprocal(out=r[:, 0:T], in_=r[:, 0:T]), reads=[b["rstd"]], writes=[b["rstd"]])
        for kc in range(DC):
            cx.op("dve", lambda e: e.scalar_tensor_tensor(
                out=dst[:, kc, 0:T], in0=src[:, kc, 0:T], scalar=t["gcol"][:, gi * DC + kc:gi * DC + kc + 1],
                in1=r[:, 0:T], op0=ALU.mult, op1=ALU.mult), reads=[srcb, b["rstd"], b["const"]], writes=[dstb])

    def load_x(s, col0, T):
        if s.planning:
            return
        c, cx, t, b = s.c, s.cx, s.t, s.b
        src = s.d["xT"].rearrange("(k p) t -> p k t", p=128)
        step = 8
        for k in range(0, c.DC, step):
            cx.dma("sp", t["xblk"][:, k:k + step, 0:T], src[:, k:k + step, col0:col0 + T], writes=[b["xblk"]])

    def mem_kv(s):
        c, cx, t, b, d = s.c, s.cx, s.t, s.b, s.d
        M = c.M
        if s.load_mem:
            if not s.planning:
                cx.dma("sp", t["KT"][:].rearrange("p f m -> p (f m)"), d["KT_i"], writes=[b["KT"]])
                cx.dma("sp", t["V"][:].rearrange("p f m -> p (f m)"), d["V_i"], writes=[b["V"]])
            return
        if not s.planning:
            src = d["memT"].rearrange("(k p) t -> p k t", p=128)
            for k in range(0, c.DC, 8):
                cx.dma("sp", t["xblk"][:, k:k + 8, 0:M], src[:, k:k + 8, :], writes=[b["xblk"]])
        s.norm(t["xblk"], b["xblk"], M, 2, t["hT"], b["hT"])
        act = lambda kk: t["hT"][:, kk, 0:M]

        def epiK(f, ps, pbuf):
            cx.op("act", lambda e: e.activation(out=t["KT"][:, f, :], in_=ps, func=AF.Copy),
                  reads=[pbuf], writes=[b["KT"]])
        s.gemm(d["w_mem_kv"], 0, c.DC, 0, c.MEMW, act, M, epiK, [b["hT"]])

        def epiV(tc, cw0, CW, ps, pbuf):
            cx.op("act", lambda e: e.activation(out=t["V"][:, tc, cw0:cw0 + CW], in_=ps, func=AF.Copy),
                  reads=[pbuf], writes=[b["V"]])
        s.gemm(d["w_mem_kv"], 0, c.DC, c.MEMW, c.MEMW, act, M, epiV, [b["hT"]], tm=True)
        if not s.planning:
            cx.dma("sp", d["KT_o"], t["KT"][:].rearrange("p f m -> p (f m)"), reads=[b["KT"]])
            cx.dma("sp", d["V_o"], t["V"][:].rearrange("p f m -> p (f m)"), reads=[b["V"]])

    def mem_attn(s, w_in, mq_c0, interleave=False):
        c, cx, nc, t, b = s.c, s.cx, s.nc, s.t, s.b
        TB = c.TB
        act = lambda kk: t["hT"][:, kk, 0:TB]

        def epi(f, ps, pbuf):
            cx.op("act", lambda e: e.activation(out=t["mqT"][:, f, :], in_=ps, func=AF.Copy),
                  reads=[pbuf], writes=[b["mqT"]])
        s.gemm(w_in, 0, c.DC, mq_c0, c.MEMW, act, TB, epi, [b["hT"]])
        if s.planning:
            return
        if interleave:
            s.gen = s.mem_steps()
        else:
            for _ in s.mem_steps():
                pass

    def pump(s):
        g = getattr(s, "gen", None)
        if g is not None:
            try:
                next(g)
            except StopIteration:
                s.gen = None

    def mem_steps(s):
        c, cx, nc, t, b = s.c, s.cx, s.nc, s.t, s.b
        TB = c.TB
        scl = 256.0 ** -0.5
        it = 0
        for mh in range(c.MEMH):
            for tc in range(c.NTC):
                par = it % 2
                it += 1
                sm = t["small"][:, par * 4:par * 4 + 4]
                smb, peb, pnb = b["small"][par], b["pexp"][par], b["pn"][par]
                pexp, pn = t["pexp"][:, par, :], t["pn"][:, par, :]
                bi = s.banks(1)[0]
                cx.acquire("pe", reads=[b["mqT"], b["KT"]], writes=[s.psb[bi]])
                for j in range(2):
                    ins = nc.tensor.matmul(s.ps[bi][:, 0:c.M], lhsT=t["mqT"][:, mh * 2 + j, tc * 128:(tc + 1) * 128],
                                           rhs=t["KT"][:, mh * 2 + j, :], start=(j == 0), stop=(j == 1))
                cx.release("pe", ins, reads=[b["mqT"], b["KT"]], writes=[s.psb[bi]])
                yield
                psr = s.ps[bi][:, 0:c.M]
                cx.op("dve", lambda e: e.reduce_max(out=sm[:, 0:1], in_=psr, axis=AX.X),
                      reads=[s.psb[bi]], writes=[smb])
                cx.op("dve", lambda e: e.tensor_scalar(out=sm[:, 1:2], in0=sm[:, 0:1], scalar1=-scl, scalar2=None,
                                                       op0=ALU.mult), reads=[smb], writes=[smb])
                cx.op("dve", lambda e: e.memset(sm[:, 2:3], 0.0), writes=[smb])
                cx.op("act", lambda e: e.activation(out=pexp, in_=psr, func=AF.Exp, bias=sm[:, 1:2],
                                                    scale=scl, accum_out=sm[:, 2:3]),
                      reads=[s.psb[bi], smb], writes=[peb, smb])
                cx.op("dve", lambda e: e.reciprocal(out=sm[:, 3:4], in_=sm[:, 2:3]),
                      reads=[smb], writes=[smb])
                cx.op("dve", lambda e: e.tensor_scalar(out=pn, in0=pexp, scalar1=sm[:, 3:4],
                                                       scalar2=None, op0=ALU.mult),
                      reads=[peb, smb], writes=[pnb])
                for mc in range(c.MC):
                    b2 = s.banks(1)[0]
                    cx.acquire("pe", reads=[pnb, b["const"]], writes=[s.psb[b2]])
                    ins = nc.tensor.transpose(s.ps[b2][:, 0:128], pn[:, mc * 128:(mc + 1) * 128], t["ident"][:])
                    cx.release("pe", ins, reads=[pnb, b["const"]], writes=[s.psb[b2]])
                    cx.op("act", lambda e: e.activation(out=t["pT"][:, mc, tc * 128:(tc + 1) * 128],
                                                        in_=s.ps[b2][:, 0:128], func=AF.Copy),
                          reads=[s.psb[b2]], writes=[b["pT"]])
                yield
            for j in range(2):
                bi = s.banks(1)[0]
                cx.acquire("pe", reads=[b["pT"], b["V"]], writes=[s.psb[bi]])
                for mc in range(c.MC):
                    ins = nc.tensor.matmul(s.ps[bi][:, 0:TB], lhsT=t["V"][:, mc, mh * 256 + j * 128:mh * 256 + (j + 1) * 128],
                                           rhs=t["pT"][:, mc, 0:TB], start=(mc == 0), stop=(mc == c.MC - 1))
                cx.release("pe", ins, reads=[b["pT"], b["V"]], writes=[s.psb[bi]])
                cx.op("act", lambda e: e.activation(out=t["catT"][:, c.TOKC + mh * 2 + j, :], in_=s.ps[bi][:, 0:TB],
                                                    func=AF.Copy), reads=[s.psb[bi]], writes=[b["catT"]])

    def pool_mixer(s, blk):
        c, cx, nc, t, b, d = s.c, s.cx, s.nc, s.t, s.b, s.d
        TB, GC = c.TB, c.GC
        W = HALO + TB
        act = lambda kk: t["hT"][:, kk, 0:TB]
        wins = (2, 4, 8, 16)

        def inproj(g):
            ug, ugb = t["ugrp"][g % 2], b["ugrp"][g % 2]
            if not s.planning:
                cx.op("dve", lambda e: e.tensor_copy(out=ug[:, :, 0:HALO], in_=t["uhalo"][:, g * GC:(g + 1) * GC, :]),
                      reads=[b["uhalo"]], writes=[ugb])

            def epi(f, ps, pbuf):
                cx.op("act", lambda e: e.activation(out=ug[:, f, HALO:W], in_=ps, func=AF.Copy),
                      reads=[pbuf], writes=[ugb])
            s.gemm(d["w_in"], 0, c.DC, g * c.G, c.G, act, TB, epi, [b["hT"]])
            if not s.planning:
                cx.op("dve", lambda e: e.tensor_copy(out=t["uhalo"][:, g * GC:(g + 1) * GC, :], in_=ug[:, :, TB:W]),
                      reads=[ugb], writes=[b["uhalo"]])

        def pooling(g):
            w = wins[g]
            ug, ugb = t["ugrp"][g % 2], b["ugrp"][g % 2]
            nst = {2: 1, 4: 2, 8: 3, 16: 4}[w]
            for fc in range(GC):
                u = ug[:, fc, :]
                cur, curb = u, ugb
                for st in range(nst):
                    sh = 1 << st
                    dst, dstb = (t["sa"], b["sa"]) if st % 2 == 0 else (t["sbb"], b["sbb"])
                    cx.op("dve", lambda e: e.tensor_tensor(out=dst[:, sh:W], in0=cur[:, sh:W], in1=cur[:, 0:W - sh],
                                                           op=ALU.add), reads=[curb], writes=[dstb])
                    cur, curb = dst[:, :], dstb
                lo = HALO if blk == 0 else 0
                cx.op("dve", lambda e: e.scalar_tensor_tensor(
                    out=t["pgrp"][:, fc, lo:TB], in0=cur[:, HALO + lo:W], scalar=1.0 / w, in1=u[:, HALO + lo:W],
                    op0=ALU.mult, op1=ALU.subtract), reads=[curb, ugb], writes=[b["pgrp"]])
                if blk == 0:
                    cx.op("dve", lambda e: e.tensor_tensor(out=t["t16"][:], in0=cur[:, HALO:2 * HALO],
                                                           in1=t["invc"][:, g, :], op=ALU.mult),
                          reads=[curb, b["const"]], writes=[b["t16"]])
                    cx.op("dve", lambda e: e.tensor_tensor(out=t["pgrp"][:, fc, 0:HALO], in0=t["t16"][:],
                                                           in1=u[:, HALO:2 * HALO], op=ALU.subtract),
                          reads=[b["t16"], ugb], writes=[b["pgrp"]])

        def grp(g):
            def epi2(f, ps, pbuf):
                cx.op("act", lambda e: e.activation(out=t["catT"][:, g * GC + f, :], in_=ps, func=AF.Copy,
                                                    scale=t["pscale"][:, g * GC + f:g * GC + f + 1]),
                      reads=[pbuf, b["const"]], writes=[b["catT"]])
            s.gemm(d["w_grp"], g * GC, GC, 0, c.G, lambda kk: t["pgrp"][:, kk, :], TB, epi2, [b["pgrp"]])

        inproj(0)
        for g in range(4):
            if g + 1 < 4:
                inproj(g + 1)
            if not s.planning:
                pooling(g)
            grp(g)

    def rope(s, src, srcb, dst, dstb, col0):
        c, cx, t, b = s.c, s.cx, s.t, s.b
        TB = c.TB
        cs = t["cos"][:, 0:TB]
        sn = t["sin"][:, 0:TB]
        ra, rb = t["ra"], t["rb"]
        cb = b["cos"]
        cx.op("dve", lambda e: e.tensor_tensor(out=ra[:], in0=src[:, 0, :], in1=cs, op=ALU.mult), reads=[srcb, cb], writes=[b["ra"]])
        cx.op("dve", lambda e: e.tensor_tensor(out=rb[:], in0=src[:, 1, :], in1=sn, op=ALU.mult), reads=[srcb, b["sin"]], writes=[b["rb"]])
        cx.op("dve", lambda e: e.tensor_tensor(out=dst[:, 0, :], in0=ra[:], in1=rb[:], op=ALU.subtract),
              reads=[b["ra"], b["rb"]], writes=[dstb])
        cx.op("dve", lambda e: e.tensor_tensor(out=ra[:], in0=src[:, 0, :], in1=sn, op=ALU.mult), reads=[srcb, b["sin"]], writes=[b["ra"]])
        cx.op("dve", lambda e: e.tensor_tensor(out=rb[:], in0=src[:, 1, :], in1=cs, op=ALU.mult), reads=[srcb, cb], writes=[b["rb"]])
        cx.op("dve", lambda e: e.tensor_tensor(out=dst[:, 1, :], in0=ra[:], in1=rb[:], op=ALU.add),
              reads=[b["ra"], b["rb"]], writes=[dstb])

    def nm(s, name, hb):
        return name if hb == 0 else name + "1"

    def ret_head_gemms(s, w_in, h):
        c, cx, nc, t, b = s.c, s.cx, s.nc, s.t, s.b
        TB = c.TB
        hb = h % 2
        kT, kTb = t[s.nm("kT", hb)], b[s.nm("kT", hb)]
        vtm, vtmb = t[s.nm("vtm", hb)], b[s.nm("vtm", hb)]
        act = lambda kk: t["hT"][:, kk, 0:TB]

        def epik(f, ps, pbuf):
            cx.op("act", lambda e: e.activation(out=kT[:, f, :], in_=ps, func=AF.Copy), reads=[pbuf], writes=[kTb])
        s.gemm(w_in, 0, c.DC, c.TOK + h * 256, 256, act, TB, epik, [b["hT"]])

        def epiv(tc, cw0, CW, ps, pbuf):
            cx.op("act", lambda e: e.activation(out=vtm[:, tc, :], in_=ps, func=AF.Copy), reads=[pbuf], writes=[vtmb])
        s.gemm(w_in, 0, c.DC, 2 * c.TOK + h * 256, 256, act, TB, epiv, [b["hT"]], tm=True)

    def ret_head_post(s, h, blk):
        c, cx, nc, t, b, d = s.c, s.cx, s.nc, s.t, s.b, s.d
        hb = h % 2
        kr, krB = t[s.nm("kr", hb)], b[s.nm("kr", hb)]
        kz, kzB = t[s.nm("kz", hb)], b[s.nm("kz", hb)]
        vtm, vtmB = t[s.nm("vtm", hb)], b[s.nm("vtm", hb)]
        krb, krbB = t[s.nm("krb", hb)], b[s.nm("krb", hb)]
        for tc in range(c.NTC):
            bi = s.banks(1)[0]
            cx.acquire("pe", reads=[krB, b["const"]], writes=[s.psb[bi]])
            for j in range(2):
                ins = nc.tensor.transpose(s.ps[bi][:, j * 128:(j + 1) * 128], kr[:, j, tc * 128:(tc + 1) * 128], t["ident"][:])
            cx.release("pe", ins, reads=[krB, b["const"]], writes=[s.psb[bi]])
            cx.op("act", lambda e: e.activation(out=kz[:, tc, :], in_=s.ps[bi][:, 0:256], func=AF.Copy,
                                                scale=t["zeta16"][:, h:h + 1]), reads=[s.psb[bi], b["const"]], writes=[kzB])
        for j in range(2):
            cx.op("act", lambda e: e.activation(out=krb[:, j, :], in_=kr[:, j, :], func=AF.Copy), reads=[krB], writes=[krbB])
        r0 = (h * c.NBLK + blk) * 128
        cx.dma("sp", d["kz_o"][r0:r0 + 128, :], kz[:].rearrange("p n e -> p (n e)"), reads=[kzB])
        cx.dma("sp", d["v_o"][r0:r0 + 128, :], vtm[:].rearrange("p n e -> p (n e)"), reads=[vtmB])
        cx.dma("sp", d["kr_o"][r0:r0 + 128, :], krb[:].rearrange("p j t -> p (j t)"), reads=[krbB])
        for n in range(c.NTC):
            s.state_update(h, n, kz, kzB, vtm, vtmB)

    def state_update(s, h, n, kz=None, kzB=None, vtm=None, vtmB=None):
        c, cx, nc, t, b = s.c, s.cx, s.nc, s.t, s.b
        if kz is None:
            kz, kzB, vtm, vtmB = t["kz"], b["kz"], t["vtm"], b["vtm"]
        for j in range(2):
            bi = s.banks(1)[0]
            cx.acquire("pe", reads=[kzB, vtmB], writes=[s.psb[bi]])
            ins = nc.tensor.matmul(s.ps[bi][:, 0:256], lhsT=kz[:, n, j * 128:(j + 1) * 128], rhs=vtm[:, n, :],
                                   start=True, stop=True)
            cx.release("pe", ins, reads=[kzB, vtmB], writes=[s.psb[bi]])
            st = t["state"][:, h, j, :]
            cx.op("dve", lambda e: e.scalar_tensor_tensor(out=st, in0=st, scalar=s.cds[h], in1=s.ps[bi][:, 0:256],
                                                          op0=ALU.mult, op1=ALU.add),
                  reads=[s.psb[bi], b["state"][h]], writes=[b["state"][h]])

    def ret_pass1(s, blk):
        c, t, b = s.c, s.t, s.b
        w = s.d["w_in2"]
        s.ret_head_gemms(w, 0)
        for h in range(c.RH):
            hb = h % 2
            if not s.planning:
                s.rope(t[s.nm("kT", hb)], b[s.nm("kT", hb)], t[s.nm("kr", hb)], b[s.nm("kr", hb)], blk * c.TB)
            if h + 1 < c.RH:
                s.ret_head_gemms(w, h + 1)
            if s.planning:
                continue
            s.ret_head_post(h, blk)

    def head_pre(s, h, blk):
        c, cx, nc, t, b, d = s.c, s.cx, s.nc, s.t, s.b, s.d
        TB = c.TB
        r0 = (h * c.NBLK + blk) * 128
        cx.dma("sp", t["kz"][:].rearrange("p n e -> p (n e)"), d["kz_i"][r0:r0 + 128, :], writes=[b["kz"]])
        cx.dma("sp", t["vtm"][:].rearrange("p n e -> p (n e)"), d["v_i"][r0:r0 + 128, :], writes=[b["vtm"]])
        cx.dma("sp", t["krb"][:].rearrange("p j t -> p (j t)"), d["kr_i"][r0:r0 + 128, :], writes=[b["krb"]])
        s.rope(t["qT"], b["qT"], t["qr"], b["qr"], blk * TB)
        for j in range(2):
            for n in range(c.NTC):
                cx.op("dve", lambda e: e.tensor_tensor(out=t["qxi"][:, j, n * 128:(n + 1) * 128],
                                                       in0=t["qr"][:, j, n * 128:(n + 1) * 128],
                                                       in1=t["xi"][:, h, :], op=ALU.mult),
                      reads=[b["qr"], b["xi"]], writes=[b["qxi"]])

    def ret_mixer(s, blk):
        c, cx, nc, t, b, d = s.c, s.cx, s.nc, s.t, s.b, s.d
        TB = c.TB
        act = lambda kk: t["hT"][:, kk, 0:TB]
        def head_gemms(h):
            def epiq(f, ps, pbuf):
                cx.op("act", lambda e: e.activation(out=t["qT"][:, f, :], in_=ps, func=AF.Copy), reads=[pbuf], writes=[b["qT"]])
            s.gemm(d["w_in"], 0, c.DC, h * 256, 256, act, TB, epiq, [b["hT"]])

            def epig(f, ps, pbuf):
                cx.op("act", lambda e: e.activation(out=t["sg"][:, h % 2, f, :], in_=ps, func=AF.Silu), reads=[pbuf], writes=[b["sg"][h % 2]])
            s.gemm(d["w_in"], 0, c.DC, 3 * c.TOK + h * 256, 256, act, TB, epig, [b["hT"]])

        head_gemms(0)
        for h in range(c.RH):
            if not s.planning:
                s.head_pre(h, blk)
            if h + 1 < c.RH:
                head_gemms(h + 1)
            if s.planning:
                continue
            for n in range(c.NTC):
                sl = slice(n * 128, (n + 1) * 128)
                bi = s.banks(1)[0]
                cx.acquire("pe", reads=[b["krb"], b["qr"]], writes=[s.psb[bi]])
                for j in range(2):
                    ins = nc.tensor.matmul(s.ps[bi][:, 0:128], lhsT=t["krb"][:, j, sl], rhs=t["qr"][:, j, sl],
                                           start=(j == 0), stop=(j == 1))
                cx.release("pe", ins, reads=[b["krb"], b["qr"]], writes=[s.psb[bi]])
                cx.op("dve", lambda e: e.scalar_tensor_tensor(out=t["sT"][:, n, :], in0=s.ps[bi][:, 0:128], scalar=1.0 / 16.0,
                                                              in1=t["maskT"][:, h, :], op0=ALU.mult, op1=ALU.mult),
                      reads=[s.psb[bi], b["maskT"]], writes=[b["sT"][n]])
                cx.op("act", lambda e: e.activation(out=t["stb"][:], in_=t["state"][:, h, :, :], func=AF.Copy),
                      reads=[b["state"][h]], writes=[b["stb"]])
                bo = s.banks(1)[0]
                rd = [b["vtm"], b["sT"][n], b["stb"], b["qxi"]]
                cx.acquire("pe", reads=rd, writes=[s.psb[bo]])
                for e_ in range(2):
                    es = slice(e_ * 128, (e_ + 1) * 128)
                    nc.tensor.matmul(s.ps[bo][:, es], lhsT=t["vtm"][:, n, es], rhs=t["sT"][:, n, :], start=True, stop=False)
                    for j in range(2):
                        ins = nc.tensor.matmul(s.ps[bo][:, es], lhsT=t["stb"][:, j, es], rhs=t["qxi"][:, j, sl],
                                               start=False, stop=(j == 1))
                cx.release("pe", ins, reads=rd, writes=[s.psb[bo]])
                for e_ in range(2):
                    cx.op("act", lambda e: e.activation(out=t["o"][:, e_, sl], in_=s.ps[bo][:, e_ * 128:(e_ + 1) * 128],
                                                        func=AF.Copy), reads=[s.psb[bo]], writes=[b["o"][n]])
                s.state_update(h, n)
            for e_ in range(2):
                cx.op("act", lambda e: e.activation(out=t["osq"][:, e_, :], in_=t["o"][:, e_, :], func=AF.Square),
                      reads=b["o"], writes=[b["osq"]])
            bi = s.banks(1)[0]
            cx.acquire("pe", reads=[b["osq"], b["const"]], writes=[s.psb[bi]])
            for e_ in range(2):
                ins = nc.tensor.matmul(s.ps[bi][:, 0:TB], lhsT=t["ones"][:], rhs=t["osq"][:, e_, :], start=(e_ == 0), stop=(e_ == 1))
            cx.release("pe", ins, reads=[b["osq"], b["const"]], writes=[s.psb[bi]])
            r = t["rstd"]
            cx.op("dve", lambda e: e.tensor_scalar(out=r[:], in0=s.ps[bi][:, 0:TB], scalar1=1.0 / 256.0, scalar2=EPS,
                                                   op0=ALU.mult, op1=ALU.add), reads=[s.psb[bi]], writes=[b["rstd"]])
            cx.op("act", lambda e: e.activation(out=r[:], in_=r[:], func=AF.Sqrt), reads=[b["rstd"]], writes=[b["rstd"]])
            cx.op("dve", lambda e: e.reciprocal(out=r[:], in_=r[:]), reads=[b["rstd"]], writes=[b["rstd"]])
            for e_ in range(2):
                cx.op("dve", lambda e: e.tensor_tensor(out=t["ra"][:], in0=t["o"][:, e_, :], in1=r[:], op=ALU.mult),
                      reads=b["o"] + [b["rstd"]], writes=[b["ra"]])
                cx.op("dve", lambda e: e.tensor_tensor(out=t["catT"][:, h * 2 + e_, :], in0=t["ra"][:], in1=t["sg"][:, h % 2, e_, :],
                                                       op=ALU.mult), reads=[b["ra"], b["sg"][h % 2]], writes=[b["catT"]])

    def out_and_ffn(s, col0, gi_ffn):
        c, cx, nc, t, b, d = s.c, s.cx, s.nc, s.t, s.b, s.d
        TB = c.TB
        if not s.planning:
            s.load_x(col0, TB)

        def epi_res(f, ps, pbuf):
            cx.op("dve", lambda e: e.tensor_tensor(out=t["xblk"][:, f, :], in0=t["xblk"][:, f, :], in1=ps, op=ALU.add),
                  reads=[pbuf, b["xblk"]], writes=[b["xblk"]])
        s.gemm(d["w_out"], 0, c.DC, 0, c.D, lambda kk: t["catT"][:, kk, 0:TB], TB, epi_res, [b["catT"]])
        s.norm(t["xblk"], b["xblk"], TB, gi_ffn, t["hT"], b["hT"])
        aT, aTb = t["catT"], b["catT"]
        for jb in range(c.NFB):
            def epi_up(f, ps, pbuf):
                cx.op("act", lambda e: e.activation(out=t["relu"][:], in_=ps, func=AF.Relu), reads=[pbuf], writes=[b["relu"]])
                cx.op("act", lambda e: e.activation(out=aT[:, f, :], in_=t["relu"][:], func=AF.Square),
                      reads=[b["relu"]], writes=[aTb])
            s.gemm(d["w_up"], 0, c.DC, jb * c.FB * 128, c.FB * 128, lambda kk: t["hT"][:, kk, 0:TB], TB, epi_up, [b["hT"]])
            s.gemm(d["w_down"], jb * c.FB, c.FB, 0, c.D, lambda kk: aT[:, kk, 0:TB], TB, epi_res, [aTb])

    def store_x(s, blk, final_gi=None):
        if s.planning:
            return
        c, cx, t, b, d = s.c, s.cx, s.t, s.b, s.d
        TB = c.TB
        dst = d["xT_out"].rearrange("(k p) t -> p k t", p=128)
        for k in range(0, c.DC, 8):
            cx.dma("sp", dst[:, k:k + 8, blk * TB:(blk + 1) * TB], t["xblk"][:, k:k + 8, :], reads=[b["xblk"]])
        if final_gi is not None:
            s.norm_stats(t["xblk"], b["xblk"], TB)
            dstn = d["xnT_out"].rearrange("(k p) t -> p k t", p=128)
            hc = c.DC // 2
            for half in range(2):
                for kk in range(hc):
                    kc = half * hc + kk
                    cx.op("dve", lambda e: e.scalar_tensor_tensor(
                        out=t["xnv"][:, kk, :], in0=t["xblk"][:, kc, :], scalar=t["gcol"][:, final_gi * c.DC + kc:final_gi * c.DC + kc + 1],
                        in1=t["rstd"][:], op0=ALU.mult, op1=ALU.mult), reads=[b["xblk"], b["rstd"], b["const"]], writes=[b["hT"]])
                cx.dma("sp", dstn[:, half * hc:(half + 1) * hc, blk * TB:(blk + 1) * TB], t["xnv"][:], reads=[b["hT"]])

    def norm_stats(s, src, srcb, T):
        c, cx, nc, t, b = s.c, s.cx, s.nc, s.t, s.b
        dst, dstb = t["hT"], b["hT"]
        for kc in range(c.DC):
            cx.op("act", lambda e: e.activation(out=dst[:, kc, 0:T], in_=src[:, kc, 0:T], func=AF.Square),
                  reads=[srcb], writes=[dstb])
        bi = s.banks(1)[0]
        cx.acquire("pe", reads=[dstb, b["const"]], writes=[s.psb[bi]])
        for kc in range(c.DC):
            ins = nc.tensor.matmul(s.ps[bi][:, 0:T], lhsT=t["ones"][:], rhs=dst[:, kc, 0:T], start=(kc == 0), stop=(kc == c.DC - 1))
        cx.release("pe", ins, reads=[dstb, b["const"]], writes=[s.psb[bi]])
        r = t["rstd"]
        cx.op("dve", lambda e: e.tensor_scalar(out=r[:, 0:T], in0=s.ps[bi][:, 0:T], scalar1=1.0 / c.D, scalar2=EPS,
                                               op0=ALU.mult, op1=ALU.add), reads=[s.psb[bi]], writes=[b["rstd"]])
        cx.op("act", lambda e: e.activation(out=r[:, 0:T], in_=r[:, 0:T], func=AF.Sqrt), reads=[b["rstd"]], writes=[b["rstd"]])
        cx.op("dve", lambda e: e.reciprocal(out=r[:, 0:T], in_=r[:, 0:T]), reads=[b["rstd"]], writes=[b["rstd"]])

    def body(s):
        c, cx, nc, t, b, d = s.c, s.cx, s.nc, s.t, s.b, s.d
        A = s.kind == "A"
        s.mem_kv()
        if A:
            if not s.planning:
                srcx = d["xT"].rearrange("(k p) t -> p k t", p=128)
                cx.dma("sp", t["xh"][:], srcx[:, :, 0:HALO], writes=[b["xh"]])
            s.load_x(HALO, c.TB)
            s.norm(t["xh"], b["xh"], HALO, 0, t["hTh"], b["hTh"])

            def epih(f, ps, pbuf):
                cx.op("act", lambda e: e.activation(out=t["uhalo"][:, f, :], in_=ps, func=AF.Copy), reads=[pbuf], writes=[b["uhalo"]])
            s.gemm(d["w_in"], 0, c.DC, 0, c.TOK, lambda kk: t["hTh"][:, kk, 0:HALO], HALO, epih, [b["hTh"]])
            for blk in range(c.NBLK):
                col0 = HALO + blk * c.TB
                if blk > 0:
                    s.load_x(col0, c.TB)
                s.norm(t["xblk"], b["xblk"], c.TB, 0, t["hT"], b["hT"])
                cx.inherit(s.tmpbufs, [b["xblk"]])
                s.mem_attn(d["w_in"], c.TOK, interleave=True)
                s.pool_mixer(blk)
                while getattr(s, "gen", None) is not None:
                    s.pump()
                cx.inherit([b["xblk"]], s.tmpbufs)
                s.out_and_ffn(col0, 1)
                s.store_x(blk)
            if not s.planning:
                for h in range(c.RH):
                    cx.op("dve", lambda e: e.memset(t["state"][:, h, :, :], 0.0), writes=[b["state"][h]])
            xo = d["xT_out"].rearrange("(k p) t -> p k t", p=128)
            for blk in range(c.NBLK):
                if not s.planning:
                    for key in cx.dslots["sp"]:
                        if cx.cnt[key] > 0:
                            cx._need("sp", (key, cx.cnt[key]))
                    for k in range(0, c.DC, 8):
                        cx.dma("sp", t["xblk"][:, k:k + 8, :], xo[:, k:k + 8, blk * c.TB:(blk + 1) * c.TB], writes=[b["xblk"]])
                s.norm(t["xblk"], b["xblk"], c.TB, 3, t["hT"], b["hT"])
                cx.inherit(s.tmpbufs, [b["xblk"]])
                s.rope_tables(blk, False)
                s.ret_pass1(blk)
                cx.inherit([b["xblk"]], s.tmpbufs)
            if not s.planning:
                lo = d["Lout"].rearrange("(h j p) e -> p h j e", p=128, j=2)
                cx.dma("sp", lo, t["state"][:], reads=b["state"])
        else:
            s.load_x(HALO, c.TB)
            s.norm(t["xblk"], b["xblk"], c.TB, 0, t["hT"], b["hT"])
            if not s.planning:
                cx.inherit(s.LstB, [b["catT"]])
                la = d["Lall"].rearrange("(r h j p) e -> p r h j e", p=128, j=2, h=c.RH)
                for h in range(c.RH):
                    Lt, Lb = s.LstT[h % 2], s.LstB[h % 2]
                    for j in range(2):
                        cx.dma("sp", Lt[:, :, j, :], la[:, :, h, j, :], writes=[Lb])
                    st = t["state"][:, h, :, :]
                    for r in range(c.NCORES):
                        wcol = t["wS"][:, r * c.RH + h:r * c.RH + h + 1]
                        if r == 0:
                            cx.op("dve", lambda e: e.tensor_scalar(out=st, in0=Lt[:, 0, :, :], scalar1=wcol, scalar2=None,
                                                                   op0=ALU.mult), reads=[Lb, b["const"]], writes=[b["state"][h]])
                        else:
                            cx.op("dve", lambda e: e.scalar_tensor_tensor(out=st, in0=Lt[:, r, :, :], scalar=wcol, in1=st,
                                                                          op0=ALU.mult, op1=ALU.add),
                                  reads=[Lb, b["const"], b["state"][h]], writes=[b["state"][h]])
                cx.inherit([b["catT"]], s.LstB)
            for blk in range(c.NBLK):
                col0 = HALO + blk * c.TB
                if blk > 0:
                    s.load_x(col0, c.TB)
                    s.norm(t["xblk"], b["xblk"], c.TB, 0, t["hT"], b["hT"])
                cx.inherit(s.tmpbufs, [b["xblk"]])
                s.rope_tables(blk, True)
                s.ret_mixer(blk)
                cx.inherit(s.membufs, s.tmpbufs)
                s.mem_attn(d["w_in"], 4 * c.TOK)
                cx.inherit([b["xblk"]], s.tmpbufs)
                s.out_and_ffn(col0, 1)
                s.store_x(blk, final_gi=3)


_PROGS = {}


def _ret_consts(cfg):
    H, C = cfg.RH, 128
    log_g = np.log1p(-(np.float32(2.0) ** (-5.0 - np.arange(H, dtype=np.float32)))).astype(np.float32)
    idx = np.arange(C, dtype=np.float32)
    diff = idx[:, None] - idx[None, :]
    decay_in = np.where(diff >= 0, np.exp(log_g[:, None, None] * np.maximum(diff, 0.0)), 0.0).astype(np.float32)
    xi = np.exp(log_g[None, :] * (idx + 1.0)[:, None]).astype(np.float32)
    zeta = np.exp(log_g[None, :] * (C - 1.0 - idx)[:, None]).astype(np.float32)
    cd = np.exp(log_g.astype(np.float64) * C)
    maskT = np.ascontiguousarray(decay_in.transpose(2, 0, 1)).reshape(C, H * C)
    xi_rep = np.ascontiguousarray(np.broadcast_to(xi.T[None, :, :], (128, H, C))).reshape(128, H * C)
    zeta16 = np.ascontiguousarray(zeta / 16.0).astype(np.float32)
    half = 128
    inv32 = (np.float32(10000.0) ** (-np.arange(half, dtype=np.float32) / np.float32(half))).astype(np.float32)
    invf = (inv32.astype(np.float64) / (2 * np.pi)).astype(np.float32).reshape(128, 1)
    return dict(maskT=maskT.astype(np.float32), xi=xi_rep.astype(np.float32), zeta16=zeta16, invf=invf), cd, log_g


def _col(g, DC):
    return np.ascontiguousarray(g.reshape(DC, 128).T)


def run_model(cfg, x, mem, positions, mix_norm_g, w_in_pool, w_pool_grp, pool_scale, w_out_pool,
              w_in_ret, w_out_ret, mem_norm_g, w_mem_kv, ffn_norm_g, w_up, w_down, final_norm_g, depth):
    NCO, TL, D = cfg.NCORES, cfg.TL, cfg.D
    consts, cd, log_g = _ret_consts(cfg)
    cds = [float(v) for v in cd]
    key = (cfg.TL, cfg.TB, cfg.DFF, cfg.NCORES)
    if key not in _PROGS:
        _PROGS[key] = (Prog(cfg, "A", cds, False), Prog(cfg, "A", cds, True), Prog(cfg, "B", cds, True))
    pA0, pA1, pB = _PROGS[key]
    cores = list(range(NCO))
    xT = np.ascontiguousarray(np.asarray(x)[0].T)
    memT = np.ascontiguousarray(np.asarray(mem)[0].T)
    pos = np.asarray(positions)[0].astype(np.int32)
    ident = np.eye(128, dtype=np.float32)
    wS = np.zeros((NCO, NCO, cfg.RH), np.float64)
    for cc in range(NCO):
        for r in range(cc):
            wS[cc, r, :] = cd ** (cfg.NCH * (cc - 1 - r))
    invc = np.zeros((NCO, 4, 16), np.float32)
    for g, w in enumerate((2, 4, 8, 16)):
        for cc in range(NCO):
            tg = cc * TL + np.arange(16)
            invc[cc, g] = 1.0 / np.minimum(tg + 1, w)

    def halo_x(xT_full, cc):
        out = np.zeros((D, HALO + TL), np.float32)
        out[:, HALO:] = xT_full[:, cc * TL:(cc + 1) * TL]
        if cc > 0:
            out[:, :HALO] = xT_full[:, cc * TL - HALO:cc * TL]
        return out

    def common(cc, xT_full, gains):
        m = dict(xT=halo_x(xT_full, cc), ident=ident,
                 gcol=np.ascontiguousarray(np.concatenate([_col(np.asarray(g), cfg.DC) for g in gains], axis=1)),
                 pos=np.ascontiguousarray(np.broadcast_to(pos[None, cc * TL:(cc + 1) * TL], (128, TL))))
        m.update(consts)
        return m

    out = None
    for j in range(depth // 2):
        ip, ir = 2 * j, 2 * j + 1
        ins = []
        for cc in cores:
            m = common(cc, xT, [mix_norm_g[ip], ffn_norm_g[ip], mem_norm_g, mix_norm_g[ir]])
            m.update(w_in=np.asarray(w_in_pool[j]), w_grp=np.asarray(w_pool_grp[j]).reshape(4 * cfg.G, cfg.G),
                     pscale=_col(np.asarray(pool_scale[j]), cfg.TOKC), w_out=np.asarray(w_out_pool[j]),
                     w_up=np.asarray(w_up[ip]), w_down=np.asarray(w_down[ip]), w_in2=np.asarray(w_in_ret[j]),
                     invc=np.ascontiguousarray(np.broadcast_to(invc[cc].reshape(1, 64), (128, 64))))
            if j == 0:
                m.update(memT=memT, w_mem_kv=np.asarray(w_mem_kv))
            else:
                m.update(KT_i=memkv[0], V_i=memkv[1])
            ins.append(m)
        res = run_bass_kernel_spmd((pA0 if j == 0 else pA1).nc, ins, core_ids=cores).results
        if j == 0:
            memkv = (res[0]["KT_o"], res[0]["V_o"])
        xT = np.concatenate([r["xT_out"] for r in res], axis=1)
        Lall = np.concatenate([r["Lout"] for r in res], axis=0)
        kvs = [(r["kz_o"], r["v_o"], r["kr_o"]) for r in res]
        ins = []
        for cc in cores:
            m = common(cc, xT, [mix_norm_g[ir], ffn_norm_g[ir], mem_norm_g, final_norm_g])
            m.update(w_in=np.asarray(w_in_ret[j]), w_out=np.asarray(w_out_ret[j]), w_up=np.asarray(w_up[ir]),
                     w_down=np.asarray(w_down[ir]), Lall=Lall, KT_i=memkv[0], V_i=memkv[1], kz_i=kvs[cc][0], v_i=kvs[cc][1], kr_i=kvs[cc][2],
                     wS=np.ascontiguousarray(np.broadcast_to(wS[cc].reshape(1, -1), (128, NCO * cfg.RH))).astype(np.float32))
            ins.append(m)
        res = run_bass_kernel_spmd(pB.nc, ins, core_ids=cores).results
        xT = np.concatenate([r["xT_out"] for r in res], axis=1)
        out = np.concatenate([r["xnT_out"] for r in res], axis=1)
    return np.ascontiguousarray(out.T)[None].astype(np.float32), np.ascontiguousarray(xT.T)[None]


def kernel(x, mem, positions, mix_norm_g, w_in_pool, w_pool_grp, pool_scale, w_out_pool,
           w_in_ret, w_out_ret, mem_norm_g, w_mem_kv, ffn_norm_g, w_up, w_down, final_norm_g):
    cfg = Cfg()
    out, _ = run_model(cfg, x, mem, positions, mix_norm_g, w_in_pool, w_pool_grp, pool_scale, w_out_pool,
                       w_in_ret, w_out_ret, mem_norm_g, w_mem_kv, ffn_norm_g, w_up, w_down, final_norm_g, 4)
    return out
```

```python
import numpy as np
import concourse.bass as bass
import concourse.mybir as mybir
from concourse.bass_utils import run_bass_kernel_spmd

F32 = mybir.dt.float32
BF16 = mybir.dt.bfloat16
I32 = mybir.dt.int32
ALU = mybir.AluOpType
AF = mybir.ActivationFunctionType
AX = mybir.AxisListType
EPS = 1e-6
HALO = 16
PI = float(np.pi)


class Cfg:
    def __init__(s, D=4096, TL=1024, TB=512, M=256, MEMH=4, DFF=16384, NCORES=8):
        s.D, s.TL, s.TB, s.M, s.MEMH, s.DFF, s.NCORES = D, TL, TB, M, MEMH, DFF, NCORES
        s.DC = D // 128
        s.MEMW = MEMH * 256
        s.MEMC = s.MEMW // 128
        s.TOK = D - s.MEMW
        s.TOKC = s.TOK // 128
        s.G = s.TOK // 4
        s.GC = s.G // 128
        s.RH = s.TOK // 256
        s.NTC = TB // 128
        s.NBLK = TL // TB
        s.MC = M // 128
        s.FB = 16
        s.NFB = DFF // (128 * s.FB)
        s.NCH = TL // 128
        s.INW = 4 * s.TOK + s.MEMW


class Buf:
    def __init__(self, name):
        self.name = name
        self.wr = {}
        self.rd = {}


class Ctx:
    def __init__(self, nc):
        self.nc = nc
        self.eng = {"pe": nc.tensor, "act": nc.scalar, "dve": nc.vector, "pool": nc.gpsimd, "sp": nc.sync}
        self.sem = {k: nc.alloc_semaphore("s_" + k) for k in ("pe", "act", "dve")}
        self.cnt = {k: 0 for k in self.sem}
        self.seen = {k: {} for k in self.eng}
        self.dslots = {}
        for q in ("pool", "sp"):
            self.dslots[q] = []
            for i in range(6):
                key = f"d_{q}{i}"
                self.sem[key] = nc.alloc_semaphore("s_" + key)
                self.cnt[key] = 0
                self.dslots[q].append(key)
        self.drr = {"pool": 0, "sp": 0}
        self.allbufs = []

    def buf(self, name):
        b = Buf(name)
        self.allbufs.append(b)
        return b

    def _need(self, e, dep):
        if dep is None:
            return
        key, c = dep
        if key == e and e == "pe":
            return
        if self.seen[e].get(key, 0) >= c:
            return
        self.seen[e][key] = c
        self.eng[e].wait_ge(self.sem[key], c)

    @staticmethod
    def _flat(bufs):
        out = []
        for x in bufs:
            if isinstance(x, (list, tuple)):
                out.extend(Ctx._flat(x))
            else:
                out.append(x)
        return out

    def acquire(self, e, reads=(), writes=()):
        reads, writes = self._flat(reads), self._flat(writes)
        for b in reads:
            for key, c in b.wr.items():
                self._need(e, (key, c))
        for b in writes:
            for key, c in b.wr.items():
                self._need(e, (key, c))
            for key, c in b.rd.items():
                self._need(e, (key, c))

    def _record(self, key, reads, writes):
        reads, writes = self._flat(reads), self._flat(writes)
        c = self.cnt[key]
        for b in reads:
            if b in writes:
                continue
            b.rd[key] = c
        for b in writes:
            b.wr[key] = c
            b.rd = {}

    def release(self, e, ins, reads=(), writes=()):
        ins.then_inc(self.sem[e], 1)
        self.cnt[e] += 1
        self._record(e, reads, writes)

    def op(self, e, fn, reads=(), writes=()):
        self.acquire(e, reads, writes)
        ins = fn(self.eng[e])
        self.release(e, ins, reads, writes)
        return ins

    def dma(self, q, out, in_, reads=(), writes=()):
        key = self.dslots[q][self.drr[q] % len(self.dslots[q])]
        self.drr[q] += 1
        if self.cnt[key] > 0:
            self._need(q, (key, self.cnt[key]))
        self.acquire(q, reads, writes)
        ins = self.eng[q].dma_start(out=out, in_=in_)
        ins.then_inc(self.sem[key], 16)
        self.cnt[key] += 16
        self._record(key, reads, writes)
        return ins

    def inherit(self, dsts, srcs):
        dsts, srcs = self._flat(dsts), self._flat(srcs)
        for d in dsts:
            for s_ in srcs:
                if s_ is d:
                    continue
                for key, c in list(s_.rd.items()) + list(s_.wr.items()):
                    if d.rd.get(key, 0) < c:
                        d.rd[key] = c

    def finish(self):
        for e in ("sp",):
            for key, c in self.cnt.items():
                if c > 0:
                    self.eng[e].wait_ge(self.sem[key], c)


class Prog:
    def __init__(s, cfg, kind, cds, load_mem=False):
        s.c = cfg
        s.kind = kind
        s.load_mem = load_mem
        s.cds = cds
        s.planning = True
        s.tiles = []
        s.build()

    def sb(s, name, shape, dt):
        n = int(np.prod(shape[1:])) * (4 if dt in (F32, I32) else 2)
        off = (s.sboff + 31) // 32 * 32
        s.sboff = off + n
        assert s.sboff <= 229300, (name, s.sboff)
        return s.nc.alloc_sbuf_tensor_at(name, list(shape), dt, offset=off)

    def sb_at(s, name, shape, dt, off):
        return s.nc.alloc_sbuf_tensor_at(name, list(shape), dt, offset=off)

    def dram_in(s, name, shape, dt=F32):
        return s.nc.dram_tensor(name, list(shape), dt, kind="ExternalInput").ap()

    def dram_out(s, name, shape, dt=F32):
        return s.nc.dram_tensor(name, list(shape), dt, kind="ExternalOutput").ap()

    def build(s):
        for planning in (True, False):
            s.planning = planning
            s.nc = bass.Bass("TRN2", target_bir_lowering=False)
            s.cx = Ctx(s.nc)
            s.sboff = 16512
            s.tile_i = 0
            s.tile_issued = 0
            if planning:
                s.tiles = []
            s.setup()
            s.body()
            if not planning:
                s.cx.finish()

    def setup(s):
        c, nc, cx = s.c, s.nc, s.cx
        A = s.kind == "A"
        d = s.d = {}
        d["xT"] = s.dram_in("xT", [c.D, HALO + c.TL])
        if s.load_mem:
            d["KT_i"] = s.dram_in("KT_i", [128, c.MEMC * c.M], BF16)
            d["V_i"] = s.dram_in("V_i", [128, c.MC * c.MEMW], BF16)
        else:
            d["memT"] = s.dram_in("memT", [c.D, c.M])
            d["KT_o"] = s.dram_out("KT_o", [128, c.MEMC * c.M], BF16)
            d["V_o"] = s.dram_out("V_o", [128, c.MC * c.MEMW], BF16)
        d["gcol"] = s.dram_in("gcol", [128, 4 * c.DC])
        d["ident"] = s.dram_in("ident", [128, 128])
        if not s.load_mem:
            d["w_mem_kv"] = s.dram_in("w_mem_kv", [c.D, 2 * c.MEMW])
        d["pos"] = s.dram_in("pos", [128, c.TL], I32)
        d["invf"] = s.dram_in("invf", [128, 1])
        d["zeta16"] = s.dram_in("zeta16", [128, c.RH])
        d["xi"] = s.dram_in("xi", [128, c.RH * 128])
        d["maskT"] = s.dram_in("maskT", [128, c.RH * 128])
        d["w_out"] = s.dram_in("w_out", [c.D, c.D])
        d["w_up"] = s.dram_in("w_up", [c.D, c.DFF])
        d["w_down"] = s.dram_in("w_down", [c.DFF, c.D])
        d["xT_out"] = s.dram_out("xT_out", [c.D, c.TL])
        if A:
            d["w_in"] = s.dram_in("w_in", [c.D, c.D])
            d["w_grp"] = s.dram_in("w_grp", [4 * c.G, c.G])
            d["pscale"] = s.dram_in("pscale", [128, c.TOKC])
            d["invc"] = s.dram_in("invc", [128, 4 * 16])
            d["w_in2"] = s.dram_in("w_in2", [c.D, c.INW])
            d["Lout"] = s.dram_out("Lout", [c.RH * 2 * 128, 256])
            d["kz_o"] = s.dram_out("kz_o", [c.RH * c.NBLK * 128, c.NTC * 256], BF16)
            d["v_o"] = s.dram_out("v_o", [c.RH * c.NBLK * 128, c.NTC * 256], BF16)
            d["kr_o"] = s.dram_out("kr_o", [c.RH * c.NBLK * 128, 2 * c.TB], BF16)
        else:
            d["w_in"] = s.dram_in("w_in", [c.D, c.INW])
            d["Lall"] = s.dram_in("Lall", [c.NCORES * c.RH * 2 * 128, 256])
            d["wS"] = s.dram_in("wS", [128, c.NCORES * c.RH])
            d["kz_i"] = s.dram_in("kz_i", [c.RH * c.NBLK * 128, c.NTC * 256], BF16)
            d["v_i"] = s.dram_in("v_i", [c.RH * c.NBLK * 128, c.NTC * 256], BF16)
            d["kr_i"] = s.dram_in("kr_i", [c.RH * c.NBLK * 128, 2 * c.TB], BF16)
            d["xnT_out"] = s.dram_out("xnT_out", [c.D, c.TL])

        TB, DC = c.TB, c.DC
        t = s.t = {}
        b = s.b = {}
        s.r1 = s.sboff
        t["xblk"] = s.sb("xblk", [128, DC, TB], F32)
        s.NSUB = DC // 8
        r1 = b["xblk"] = [cx.buf(f"xblk{i}") for i in range(s.NSUB)]
        s.hT_off = (s.sboff + 31) // 32 * 32
        t["hT"] = s.sb("hT", [128, DC, TB], BF16)
        t["xnv"] = s.sb_at("xnv", [128, DC // 2, TB], F32, s.hT_off)
        b["hT"] = [cx.buf(f"hT{i}") for i in range(s.NSUB)]
        s.hTb = lambda kh: [b["hT"][kh]]
        s.catT_off = (s.sboff + 31) // 32 * 32
        t["catT"] = s.sb("catT", [128, DC, TB], BF16)
        b["catT"] = cx.buf("catT")
        s.NWB = 4
        t["wb"] = [s.sb(f"wb{i}", [128, 4096], BF16) for i in range(s.NWB)]
        b["wb"] = [cx.buf(f"wb{i}") for i in range(s.NWB)]
        t["rstd"] = s.sb("rstd", [128, TB], F32)
        b["rstd"] = cx.buf("rstd")
        t["ones"] = s.sb("ones", [128, 128], BF16)
        t["ident"] = s.sb("ident", [128, 128], F32)
        t["gcol"] = s.sb("gcol", [128, 4 * DC], F32)
        t["KT"] = s.sb("KT", [128, c.MEMC, c.M], BF16)
        t["V"] = s.sb("V", [128, c.MC, c.MEMW], BF16)
        b["KT"] = cx.buf("KT")
        b["V"] = cx.buf("V")
        b["const"] = cx.buf("const")
        t["state"] = s.sb("state", [128, c.RH, 2, 256], F32)
        b["state"] = [cx.buf(f"state{h}") for h in range(c.RH)]
        t["zeta16"] = s.sb("zeta16", [128, c.RH], F32)
        t["invf"] = s.sb("invf", [128, 1], F32)
        t["small"] = s.sb("small", [128, 16], F32)
        b["small"] = cx.buf("small")
        t["relu"] = s.sb("relu", [128, TB], F32)
        b["relu"] = cx.buf("relu")
        if A:
            t["pscale"] = s.sb("pscale", [128, c.TOKC], F32)
            t["invc"] = s.sb("invc", [128, 4, 16], F32)
            t["uhalo"] = s.sb("uhalo", [128, c.TOKC, 16], F32)
            b["uhalo"] = cx.buf("uhalo")
            t["xh"] = s.sb("xh", [128, DC, HALO], F32)
            b["xh"] = cx.buf("xh")
            t["hTh"] = s.sb("hTh", [128, DC, HALO], BF16)
            b["hTh"] = cx.buf("hTh")
        else:
            t["wS"] = s.sb("wS", [128, c.NCORES * c.RH], F32)
        s.ps = [nc.alloc_psum_tensor(f"ps{i}", [128, 512], F32) for i in range(8)]
        s.psb = [cx.buf(f"ps{i}") for i in range(8)]
        s.bank_rr = 0
        s.grp_rr = 0

        o = s.r1
        s.tmpbufs = []

        def carve(name, shape, dt, at=None):
            nonlocal o
            if at is not None:
                o = at
            n = int(np.prod(shape[1:])) * (4 if dt in (F32, I32) else 2)
            o = (o + 31) // 32 * 32
            h = s.sb_at(name, list(shape), dt, o)
            o += n
            assert o <= s.r1 + DC * TB * 4, (name, o - s.r1)
            t[name] = h
            b[name] = cx.buf(name)
            s.tmpbufs.append(b[name])
            return h
        W = HALO + TB
        carve("cos", [128, TB], F32)
        carve("sin", [128, TB], F32)
        carve("posi", [128, TB], I32)
        carve("posf", [128, TB], F32)
        if not A:
            carve("xi", [128, c.RH, 128], F32)
            carve("maskT", [128, c.RH, 128], F32)
        o_t = o
        carve("mqT", [128, c.MEMC, TB], BF16)
        carve("pexp", [128, 2, c.M], F32)
        carve("pn", [128, 2, c.M], F32)
        carve("pT", [128, c.MC, TB], BF16)
        if A:
            o_pool = o
            carve("ugrp", [128, c.GC, W], F32)
            carve("ugrp1", [128, c.GC, W], F32)
            carve("pgrp", [128, c.GC, TB], BF16)
            carve("sa", [128, W], F32)
            carve("sbb", [128, W], F32)
            carve("t16", [128, 16], F32)
            o = o_pool
            carve("kT", [128, 2, TB], F32)
            carve("kr", [128, 2, TB], F32)
            carve("kT1", [128, 2, TB], F32)
            carve("kr1", [128, 2, TB], F32)
        else:
            o = o_t
        carve("ra", [128, TB], F32)
        carve("rb", [128, TB], F32)
        carve("vtm", [128, c.NTC, 256], BF16)
        carve("kz", [128, c.NTC, 256], BF16)
        carve("krb", [128, 2, TB], BF16)
        if A:
            carve("vtm1", [128, c.NTC, 256], BF16)
            carve("kz1", [128, c.NTC, 256], BF16)
            carve("krb1", [128, 2, TB], BF16)
        if not A:
            carve("qT", [128, 2, TB], F32)
            carve("qr", [128, 2, TB], BF16)
            carve("qxi", [128, 2, TB], BF16)
            carve("sg", [128, 2, 2, TB], F32)
            carve("sT", [128, c.NTC, 128], BF16)
            carve("stb", [128, 2, 256], BF16)
            carve("o", [128, 2, TB], F32)
            carve("osq", [128, 2, TB], BF16)

        s.membufs = [b[n] for n in ("mqT", "pexp", "pn", "pT")]
        b["pexp"] = [b["pexp"], cx.buf("pexp1")]
        b["pn"] = [b["pn"], cx.buf("pn1")]
        b["small"] = [b["small"], cx.buf("small1")]
        s.membufs += [b["pexp"][1], b["pn"][1]]
        s.tmpbufs += [b["pexp"][1], b["pn"][1]]
        if A:
            t["ugrp"] = [t["ugrp"], t["ugrp1"]]
            b["ugrp"] = [b["ugrp"], b["ugrp1"]]
        if not A:
            s.LstT = [s.sb_at(f"Lst{i}", [128, c.NCORES, 2, 256], F32, s.catT_off + i * c.NCORES * 2 * 256 * 4) for i in range(2)]
            s.LstB = [cx.buf("LstA"), cx.buf("LstB")]
            b["sg"] = [b["sg"], cx.buf("sg1")]
            s.tmpbufs.append(b["sg"][1])
            b["sT"] = [b["sT"]] + [cx.buf(f"sT{n}") for n in range(1, c.NTC)]
            b["o"] = [b["o"]] + [cx.buf(f"o{n}") for n in range(1, c.NTC)]
            s.tmpbufs += b["sT"][1:] + b["o"][1:]
        if s.planning:
            return
        cb = b["const"]
        cx.op("dve", lambda e: e.memset(t["ones"][:], 1.0), writes=[cb])
        cx.dma("sp", t["ident"][:], d["ident"], writes=[cb])
        cx.dma("sp", t["gcol"][:], d["gcol"], writes=[cb])
        cx.dma("sp", t["zeta16"][:], d["zeta16"], writes=[cb])
        cx.dma("sp", t["invf"][:], d["invf"], writes=[cb])
        if A:
            cx.dma("sp", t["pscale"][:], d["pscale"], writes=[cb])
            cx.dma("sp", t["invc"][:], d["invc"].rearrange("p (g t) -> p g t", t=16), writes=[cb])
        else:
            cx.dma("sp", t["wS"][:], d["wS"], writes=[cb])

    def rope_tables(s, blk, with_xi):
        if s.planning:
            return
        c, cx, t, b, d = s.c, s.cx, s.t, s.b, s.d
        TB = c.TB
        cb = b["const"]
        cx.dma("sp", t["posi"][:], d["pos"][:, blk * TB:(blk + 1) * TB], writes=[b["posi"]])
        if with_xi:
            cx.dma("sp", t["xi"][:], d["xi"].rearrange("p (h t) -> p h t", t=128), writes=[b["xi"]])
            cx.dma("sp", t["maskT"][:], d["maskT"].rearrange("p (h t) -> p h t", t=128), writes=[b["maskT"]])
        posf, posi, ra, rb = t["posf"], t["posi"], t["ra"], t["rb"]
        bpi, bpf, bra, brb = b["posi"], b["posf"], b["ra"], b["rb"]
        cx.op("dve", lambda e: e.tensor_copy(out=posf[:], in_=posi[:]), reads=[bpi], writes=[bpf])
        for name, off in (("sin", 0.0), ("cos", 0.25)):
            o_, ob = t[name], b[name]
            cx.op("dve", lambda e: e.tensor_scalar(out=o_[:], in0=posf[:], scalar1=t["invf"][:, 0:1], scalar2=off,
                                                   op0=ALU.mult, op1=ALU.add), reads=[bpf, cb], writes=[ob])
            cx.op("dve", lambda e: e.tensor_copy(out=posi[:], in_=o_[:]), reads=[ob], writes=[bpi])
            cx.op("dve", lambda e: e.tensor_copy(out=ra[:], in_=posi[:]), reads=[bpi], writes=[bra])
            cx.op("dve", lambda e: e.tensor_tensor(out=o_[:], in0=o_[:], in1=ra[:], op=ALU.subtract), reads=[ob, bra], writes=[ob])
            cx.op("dve", lambda e: e.tensor_scalar(out=rb[:], in0=o_[:], scalar1=0.5, scalar2=None, op0=ALU.is_gt),
                  reads=[ob], writes=[brb])
            cx.op("dve", lambda e: e.tensor_tensor(out=o_[:], in0=o_[:], in1=rb[:], op=ALU.subtract), reads=[ob, brb], writes=[ob])
            cx.op("dve", lambda e: e.tensor_scalar(out=rb[:], in0=o_[:], scalar1=-0.5, scalar2=None, op0=ALU.is_lt),
                  reads=[ob], writes=[brb])
            cx.op("dve", lambda e: e.tensor_tensor(out=o_[:], in0=o_[:], in1=rb[:], op=ALU.add), reads=[ob, brb], writes=[ob])
            cx.op("act", lambda e: e.activation(out=o_[:], in_=o_[:], func=AF.Sin, scale=6.28318), reads=[ob], writes=[ob])

    def banks(s, n):
        if n == 4:
            g = s.grp_rr % 2
            s.grp_rr += 1
            idx = list(range(4 * g, 4 * g + 4))
        else:
            idx = []
            for _ in range(n):
                idx.append(s.bank_rr % 8)
                s.bank_rr += 1
        return idx

    def issue_tiles(s, upto):
        cx = s.cx
        while s.tile_issued < min(upto, len(s.tiles)):
            i = s.tile_issued
            W, k0, KCt, c0, CW = s.tiles[i]
            slot = i % s.NWB
            dst = s.t["wb"][slot][:, 0:KCt * CW].rearrange("p (k c) -> p k c", c=CW)
            src = W[k0 * 128:(k0 + KCt) * 128, c0:c0 + CW].rearrange("(k p) c -> p k c", p=128)
            cx.dma("pool", dst, src, writes=[s.b["wb"][slot]])
            s.tile_issued += 1

    def gemm(s, W, k0, KC, c0, ncols, act, T, epi, abufs, tm=False):
        cx, nc = s.cx, s.nc
        KT_ = 8 if (KC >= 8 and KC % 8 == 0) else KC
        assert KC % KT_ == 0
        nkh = KC // KT_
        col = 0
        while col < ncols:
            CW = min(512, ncols - col)
            tiles = []
            for kh in range(nkh):
                if s.planning:
                    s.tiles.append((W, k0 + kh * KT_, KT_, c0 + col, CW))
                tiles.append(s.tile_i)
                s.tile_i += 1
            if s.planning:
                col += CW
                continue
            nacc = (CW // 128) if not tm else (T // 128)
            bidx = s.banks(4) if nacc > 2 else s.banks(nacc)
            bidx = bidx[:nacc]
            pb = [s.psb[i] for i in bidx]
            for kh in range(nkh):
                ti = tiles[kh]
                s.issue_tiles(ti + s.NWB)
                slot = ti % s.NWB
                wbuf = s.b["wb"][slot]
                wt = s.t["wb"][slot][:, 0:KT_ * CW].rearrange("p (k c) -> p k c", c=CW)
                ab = abufs(kh) if callable(abufs) else abufs
                cx.acquire("pe", reads=[wbuf] + ab, writes=pb if kh == 0 else [])
                ins = None
                for a in range(nacc):
                    for kc in range(KT_):
                        kk = kh * KT_ + kc
                        st, sp_ = (kk == 0), (kk == KC - 1)
                        if not tm:
                            ins = nc.tensor.matmul(s.ps[bidx[a]][:, 0:T], lhsT=wt[:, kc, a * 128:(a + 1) * 128],
                                                   rhs=act(kk), start=st, stop=sp_)
                        else:
                            ins = nc.tensor.matmul(s.ps[bidx[a]][:, 0:CW], lhsT=act(kk)[:, a * 128:(a + 1) * 128],
                                                   rhs=wt[:, kc, :], start=st, stop=sp_)
                last = kh == nkh - 1
                cx.release("pe", ins, reads=[wbuf] + ab, writes=pb if last else [])
            for a in range(nacc):
                if not tm:
                    epi((col // 128) + a, s.ps[bidx[a]][:, 0:T], pb[a])
                else:
                    epi(a, col, CW, s.ps[bidx[a]][:, 0:CW], pb[a])
            col += CW
            s.pump()

    def norm(s, src, srcb, T, gi, dst, dstb, sq_scratch=None):
        if s.planning:
            return
        c, cx, nc, t, b = s.c, s.cx, s.nc, s.t, s.b
        DC = c.DC

        def sub(bb, kc):
            return bb[kc // 8] if isinstance(bb, list) else bb
        for kc in range(DC):
            cx.op("act", lambda e: e.activation(out=dst[:, kc, 0:T], in_=src[:, kc, 0:T], func=AF.Square),
                  reads=[sub(srcb, kc)], writes=[sub(dstb, kc)])
        bi = s.banks(1)[0]
        for kc in range(DC):
            if kc % 8 == 0:
                cx.acquire("pe", reads=[sub(dstb, kc), b["const"]], writes=[s.psb[bi]] if kc == 0 else [])
            ins = nc.tensor.matmul(s.ps[bi][:, 0:T], lhsT=t["ones"][:], rhs=dst[:, kc, 0:T],
                                   start=(kc == 0), stop=(kc == DC - 1))
        cx.release("pe", ins, reads=[dstb, b["const"]], writes=[s.psb[bi]])
        r = t["rstd"]
        cx.op("dve", lambda e: e.tensor_scalar(out=r[:, 0:T], in0=s.ps[bi][:, 0:T], scalar1=1.0 / c.D, scalar2=EPS,
                                               op0=ALU.mult, op1=ALU.add), reads=[s.psb[bi]], writes=[b["rstd"]])
        cx.op("act", lambda e: e.activation(out=r[:, 0:T], in_=r[:, 0:T], func=AF.Sqrt), reads=[b["rstd"]], writes=[b["rstd"]])
        cx.op("dve", lambda e: e.reciprocal(out=r[:, 0:T], in_=r[:, 0:T]), reads=[b["rstd"]], writes=[b["rstd"]])
        for kc in range(DC):
            cx.op("dve", lambda e: e.scalar_tensor_tensor(
                out=dst[:, kc, 0:T], in0=src[:, kc, 0:T], scalar=t["gcol"][:, gi * DC + kc:gi * DC + kc + 1],
                in1=r[:, 0:T], op0=ALU.mult, op1=ALU.mult), reads=[sub(srcb, kc), b["rstd"], b["const"]], writes=[sub(dstb, kc)])

    def load_x(s, col0, T):
        if s.planning:
            return
        c, cx, t, b = s.c, s.cx, s.t, s.b
        src = s.d["xT"].rearrange("(k p) t -> p k t", p=128)
        step = 8
        for k in range(0, c.DC, step):
            cx.dma("sp", t["xblk"][:, k:k + step, 0:T], src[:, k:k + step, col0:col0 + T], writes=[b["xblk"][k // 8]])

    def mem_kv(s):
        c, cx, t, b, d = s.c, s.cx, s.t, s.b, s.d
        M = c.M
        if s.load_mem:
            if not s.planning:
                cx.dma("sp", t["KT"][:].rearrange("p f m -> p (f m)"), d["KT_i"], writes=[b["KT"]])
                cx.dma("sp", t["V"][:].rearrange("p f m -> p (f m)"), d["V_i"], writes=[b["V"]])
            return
        if not s.planning:
            src = d["memT"].rearrange("(k p) t -> p k t", p=128)
            for k in range(0, c.DC, 8):
                cx.dma("sp", t["xblk"][:, k:k + 8, 0:M], src[:, k:k + 8, :], writes=[b["xblk"]])
        s.norm(t["xblk"], b["xblk"], M, 2, t["hT"], b["hT"])
        act = lambda kk: t["hT"][:, kk, 0:M]

        def epiK(f, ps, pbuf):
            cx.op("act", lambda e: e.activation(out=t["KT"][:, f, :], in_=ps, func=AF.Copy),
                  reads=[pbuf], writes=[b["KT"]])
        s.gemm(d["w_mem_kv"], 0, c.DC, 0, c.MEMW, act, M, epiK, [b["hT"]])

        def epiV(tc, cw0, CW, ps, pbuf):
            cx.op("act", lambda e: e.activation(out=t["V"][:, tc, cw0:cw0 + CW], in_=ps, func=AF.Copy),
                  reads=[pbuf], writes=[b["V"]])
        s.gemm(d["w_mem_kv"], 0, c.DC, c.MEMW, c.MEMW, act, M, epiV, [b["hT"]], tm=True)
        if not s.planning:
            cx.dma("sp", d["KT_o"], t["KT"][:].rearrange("p f m -> p (f m)"), reads=[b["KT"]])
            cx.dma("sp", d["V_o"], t["V"][:].rearrange("p f m -> p (f m)"), reads=[b["V"]])

    def mem_attn(s, w_in, mq_c0, interleave=False):
        c, cx, nc, t, b = s.c, s.cx, s.nc, s.t, s.b
        TB = c.TB
        act = lambda kk: t["hT"][:, kk, 0:TB]

        def epi(f, ps, pbuf):
            cx.op("act", lambda e: e.activation(out=t["mqT"][:, f, :], in_=ps, func=AF.Copy),
                  reads=[pbuf], writes=[b["mqT"]])
        s.gemm(w_in, 0, c.DC, mq_c0, c.MEMW, act, TB, epi, s.hTb)
        if s.planning:
            return
        if interleave:
            s.gen = s.mem_steps()
        else:
            for _ in s.mem_steps():
                pass

    def pump(s):
        g = getattr(s, "gen", None)
        if g is not None:
            try:
                next(g)
            except StopIteration:
                s.gen = None

    def mem_steps(s):
        c, cx, nc, t, b = s.c, s.cx, s.nc, s.t, s.b
        TB = c.TB
        scl = 256.0 ** -0.5
        it = 0
        for mh in range(c.MEMH):
            for tc in range(c.NTC):
                par = it % 2
                it += 1
                sm = t["small"][:, par * 4:par * 4 + 4]
                smb, peb, pnb = b["small"][par], b["pexp"][par], b["pn"][par]
                pexp, pn = t["pexp"][:, par, :], t["pn"][:, par, :]
                bi = s.banks(1)[0]
                cx.acquire("pe", reads=[b["mqT"], b["KT"]], writes=[s.psb[bi]])
                for j in range(2):
                    ins = nc.tensor.matmul(s.ps[bi][:, 0:c.M], lhsT=t["mqT"][:, mh * 2 + j, tc * 128:(tc + 1) * 128],
                                           rhs=t["KT"][:, mh * 2 + j, :], start=(j == 0), stop=(j == 1))
                cx.release("pe", ins, reads=[b["mqT"], b["KT"]], writes=[s.psb[bi]])
                yield
                psr = s.ps[bi][:, 0:c.M]
                cx.op("dve", lambda e: e.reduce_max(out=sm[:, 0:1], in_=psr, axis=AX.X),
                      reads=[s.psb[bi]], writes=[smb])
                cx.op("dve", lambda e: e.tensor_scalar(out=sm[:, 1:2], in0=sm[:, 0:1], scalar1=-scl, scalar2=None,
                                                       op0=ALU.mult), reads=[smb], writes=[smb])
                cx.op("dve", lambda e: e.memset(sm[:, 2:3], 0.0), writes=[smb])
                cx.op("act", lambda e: e.activation(out=pexp, in_=psr, func=AF.Exp, bias=sm[:, 1:2],
                                                    scale=scl, accum_out=sm[:, 2:3]),
                      reads=[s.psb[bi], smb], writes=[peb, smb])
                cx.op("dve", lambda e: e.reciprocal(out=sm[:, 3:4], in_=sm[:, 2:3]),
                      reads=[smb], writes=[smb])
                cx.op("dve", lambda e: e.tensor_scalar(out=pn, in0=pexp, scalar1=sm[:, 3:4],
                                                       scalar2=None, op0=ALU.mult),
                      reads=[peb, smb], writes=[pnb])
                for mc in range(c.MC):
                    b2 = s.banks(1)[0]
                    cx.acquire("pe", reads=[pnb, b["const"]], writes=[s.psb[b2]])
                    ins = nc.tensor.transpose(s.ps[b2][:, 0:128], pn[:, mc * 128:(mc + 1) * 128], t["ident"][:])
                    cx.release("pe", ins, reads=[pnb, b["const"]], writes=[s.psb[b2]])
                    cx.op("act", lambda e: e.activation(out=t["pT"][:, mc, tc * 128:(tc + 1) * 128],
                                                        in_=s.ps[b2][:, 0:128], func=AF.Copy),
                          reads=[s.psb[b2]], writes=[b["pT"]])
                yield
            for j in range(2):
                bi = s.banks(1)[0]
                cx.acquire("pe", reads=[b["pT"], b["V"]], writes=[s.psb[bi]])
                for mc in range(c.MC):
                    ins = nc.tensor.matmul(s.ps[bi][:, 0:TB], lhsT=t["V"][:, mc, mh * 256 + j * 128:mh * 256 + (j + 1) * 128],
                                           rhs=t["pT"][:, mc, 0:TB], start=(mc == 0), stop=(mc == c.MC - 1))
                cx.release("pe", ins, reads=[b["pT"], b["V"]], writes=[s.psb[bi]])
                cx.op("act", lambda e: e.activation(out=t["catT"][:, c.TOKC + mh * 2 + j, :], in_=s.ps[bi][:, 0:TB],
                                                    func=AF.Copy), reads=[s.psb[bi]], writes=[b["catT"]])

    def pool_mixer(s, blk):
        c, cx, nc, t, b, d = s.c, s.cx, s.nc, s.t, s.b, s.d
        TB, GC = c.TB, c.GC
        W = HALO + TB
        act = lambda kk: t["hT"][:, kk, 0:TB]
        wins = (2, 4, 8, 16)

        def inproj(g):
            ug, ugb = t["ugrp"][g % 2], b["ugrp"][g % 2]
            if not s.planning:
                cx.op("dve", lambda e: e.tensor_copy(out=ug[:, :, 0:HALO], in_=t["uhalo"][:, g * GC:(g + 1) * GC, :]),
                      reads=[b["uhalo"]], writes=[ugb])

            def epi(f, ps, pbuf):
                cx.op("act", lambda e: e.activation(out=ug[:, f, HALO:W], in_=ps, func=AF.Copy),
                      reads=[pbuf], writes=[ugb])
            s.gemm(d["w_in"], 0, c.DC, g * c.G, c.G, act, TB, epi, s.hTb)
            if not s.planning:
                cx.op("dve", lambda e: e.tensor_copy(out=t["uhalo"][:, g * GC:(g + 1) * GC, :], in_=ug[:, :, TB:W]),
                      reads=[ugb], writes=[b["uhalo"]])

        def pooling(g):
            w = wins[g]
            ug, ugb = t["ugrp"][g % 2], b["ugrp"][g % 2]
            nst = {2: 1, 4: 2, 8: 3, 16: 4}[w]
            for fc in range(GC):
                u = ug[:, fc, :]
                cur, curb = u, ugb
                for st in range(nst):
                    sh = 1 << st
                    dst, dstb = (t["sa"], b["sa"]) if st % 2 == 0 else (t["sbb"], b["sbb"])
                    cx.op("dve", lambda e: e.tensor_tensor(out=dst[:, sh:W], in0=cur[:, sh:W], in1=cur[:, 0:W - sh],
                                                           op=ALU.add), reads=[curb], writes=[dstb])
                    cur, curb = dst[:, :], dstb
                lo = HALO if blk == 0 else 0
                cx.op("dve", lambda e: e.scalar_tensor_tensor(
                    out=t["pgrp"][:, fc, lo:TB], in0=cur[:, HALO + lo:W], scalar=1.0 / w, in1=u[:, HALO + lo:W],
                    op0=ALU.mult, op1=ALU.subtract), reads=[curb, ugb], writes=[b["pgrp"]])
                if blk == 0:
                    cx.op("dve", lambda e: e.tensor_tensor(out=t["t16"][:], in0=cur[:, HALO:2 * HALO],
                                                           in1=t["invc"][:, g, :], op=ALU.mult),
                          reads=[curb, b["const"]], writes=[b["t16"]])
                    cx.op("dve", lambda e: e.tensor_tensor(out=t["pgrp"][:, fc, 0:HALO], in0=t["t16"][:],
                                                           in1=u[:, HALO:2 * HALO], op=ALU.subtract),
                          reads=[b["t16"], ugb], writes=[b["pgrp"]])

        def grp(g):
            def epi2(f, ps, pbuf):
                cx.op("act", lambda e: e.activation(out=t["catT"][:, g * GC + f, :], in_=ps, func=AF.Copy,
                                                    scale=t["pscale"][:, g * GC + f:g * GC + f + 1]),
                      reads=[pbuf, b["const"]], writes=[b["catT"]])
            s.gemm(d["w_grp"], g * GC, GC, 0, c.G, lambda kk: t["pgrp"][:, kk, :], TB, epi2, [b["pgrp"]])

        inproj(0)
        for g in range(4):
            if g + 1 < 4:
                inproj(g + 1)
            if not s.planning:
                pooling(g)
            grp(g)

    def rope(s, src, srcb, dst, dstb, col0):
        c, cx, t, b = s.c, s.cx, s.t, s.b
        TB = c.TB
        cs = t["cos"][:, 0:TB]
        sn = t["sin"][:, 0:TB]
        ra, rb = t["ra"], t["rb"]
        cb = b["cos"]
        cx.op("dve", lambda e: e.tensor_tensor(out=ra[:], in0=src[:, 0, :], in1=cs, op=ALU.mult), reads=[srcb, cb], writes=[b["ra"]])
        cx.op("dve", lambda e: e.tensor_tensor(out=rb[:], in0=src[:, 1, :], in1=sn, op=ALU.mult), reads=[srcb, b["sin"]], writes=[b["rb"]])
        cx.op("dve", lambda e: e.tensor_tensor(out=dst[:, 0, :], in0=ra[:], in1=rb[:], op=ALU.subtract),
              reads=[b["ra"], b["rb"]], writes=[dstb])
        cx.op("dve", lambda e: e.tensor_tensor(out=ra[:], in0=src[:, 0, :], in1=sn, op=ALU.mult), reads=[srcb, b["sin"]], writes=[b["ra"]])
        cx.op("dve", lambda e: e.tensor_tensor(out=rb[:], in0=src[:, 1, :], in1=cs, op=ALU.mult), reads=[srcb, cb], writes=[b["rb"]])
        cx.op("dve", lambda e: e.tensor_tensor(out=dst[:, 1, :], in0=ra[:], in1=rb[:], op=ALU.add),
              reads=[b["ra"], b["rb"]], writes=[dstb])

    def nm(s, name, hb):
        return name if hb == 0 else name + "1"

    def ret_head_gemms(s, w_in, h):
        c, cx, nc, t, b = s.c, s.cx, s.nc, s.t, s.b
        TB = c.TB
        hb = h % 2
        kT, kTb = t[s.nm("kT", hb)], b[s.nm("kT", hb)]
        vtm, vtmb = t[s.nm("vtm", hb)], b[s.nm("vtm", hb)]
        act = lambda kk: t["hT"][:, kk, 0:TB]

        def epik(f, ps, pbuf):
            cx.op("act", lambda e: e.activation(out=kT[:, f, :], in_=ps, func=AF.Copy), reads=[pbuf], writes=[kTb])
        s.gemm(w_in, 0, c.DC, c.TOK + h * 256, 256, act, TB, epik, s.hTb)

        def epiv(tc, cw0, CW, ps, pbuf):
            cx.op("act", lambda e: e.activation(out=vtm[:, tc, :], in_=ps, func=AF.Copy), reads=[pbuf], writes=[vtmb])
        s.gemm(w_in, 0, c.DC, 2 * c.TOK + h * 256, 256, act, TB, epiv, s.hTb, tm=True)

    def ret_head_post(s, h, blk):
        c, cx, nc, t, b, d = s.c, s.cx, s.nc, s.t, s.b, s.d
        hb = h % 2
        kr, krB = t[s.nm("kr", hb)], b[s.nm("kr", hb)]
        kz, kzB = t[s.nm("kz", hb)], b[s.nm("kz", hb)]
        vtm, vtmB = t[s.nm("vtm", hb)], b[s.nm("vtm", hb)]
        krb, krbB = t[s.nm("krb", hb)], b[s.nm("krb", hb)]
        for tc in range(c.NTC):
            bi = s.banks(1)[0]
            cx.acquire("pe", reads=[krB, b["const"]], writes=[s.psb[bi]])
            for j in range(2):
                ins = nc.tensor.transpose(s.ps[bi][:, j * 128:(j + 1) * 128], kr[:, j, tc * 128:(tc + 1) * 128], t["ident"][:])
            cx.release("pe", ins, reads=[krB, b["const"]], writes=[s.psb[bi]])
            cx.op("act", lambda e: e.activation(out=kz[:, tc, :], in_=s.ps[bi][:, 0:256], func=AF.Copy,
                                                scale=t["zeta16"][:, h:h + 1]), reads=[s.psb[bi], b["const"]], writes=[kzB])
        for j in range(2):
            cx.op("act", lambda e: e.activation(out=krb[:, j, :], in_=kr[:, j, :], func=AF.Copy), reads=[krB], writes=[krbB])
        r0 = (h * c.NBLK + blk) * 128
        cx.dma("sp", d["kz_o"][r0:r0 + 128, :], kz[:].rearrange("p n e -> p (n e)"), reads=[kzB])
        cx.dma("sp", d["v_o"][r0:r0 + 128, :], vtm[:].rearrange("p n e -> p (n e)"), reads=[vtmB])
        cx.dma("sp", d["kr_o"][r0:r0 + 128, :], krb[:].rearrange("p j t -> p (j t)"), reads=[krbB])
        for n in range(c.NTC):
            s.state_update(h, n, kz, kzB, vtm, vtmB)

    def state_update(s, h, n, kz=None, kzB=None, vtm=None, vtmB=None):
        c, cx, nc, t, b = s.c, s.cx, s.nc, s.t, s.b
        if kz is None:
            kz, kzB, vtm, vtmB = t["kz"], b["kz"], t["vtm"], b["vtm"]
        for j in range(2):
            bi = s.banks(1)[0]
            cx.acquire("pe", reads=[kzB, vtmB], writes=[s.psb[bi]])
            ins = nc.tensor.matmul(s.ps[bi][:, 0:256], lhsT=kz[:, n, j * 128:(j + 1) * 128], rhs=vtm[:, n, :],
                                   start=True, stop=True)
            cx.release("pe", ins, reads=[kzB, vtmB], writes=[s.psb[bi]])
            st = t["state"][:, h, j, :]
            cx.op("dve", lambda e: e.scalar_tensor_tensor(out=st, in0=st, scalar=s.cds[h], in1=s.ps[bi][:, 0:256],
                                                          op0=ALU.mult, op1=ALU.add),
                  reads=[s.psb[bi], b["state"][h]], writes=[b["state"][h]])

    def ret_pass1(s, blk):
        c, t, b = s.c, s.t, s.b
        w = s.d["w_in2"]
        s.ret_head_gemms(w, 0)
        for h in range(c.RH):
            hb = h % 2
            if not s.planning:
                s.rope(t[s.nm("kT", hb)], b[s.nm("kT", hb)], t[s.nm("kr", hb)], b[s.nm("kr", hb)], blk * c.TB)
            if h + 1 < c.RH:
                s.ret_head_gemms(w, h + 1)
            if s.planning:
                continue
            s.ret_head_post(h, blk)

    def head_pre(s, h, blk):
        c, cx, nc, t, b, d = s.c, s.cx, s.nc, s.t, s.b, s.d
        TB = c.TB
        r0 = (h * c.NBLK + blk) * 128
        cx.dma("sp", t["kz"][:].rearrange("p n e -> p (n e)"), d["kz_i"][r0:r0 + 128, :], writes=[b["kz"]])
        cx.dma("sp", t["vtm"][:].rearrange("p n e -> p (n e)"), d["v_i"][r0:r0 + 128, :], writes=[b["vtm"]])
        cx.dma("sp", t["krb"][:].rearrange("p j t -> p (j t)"), d["kr_i"][r0:r0 + 128, :], writes=[b["krb"]])
        s.rope(t["qT"], b["qT"], t["qr"], b["qr"], blk * TB)
        for j in range(2):
            for n in range(c.NTC):
                cx.op("dve", lambda e: e.tensor_tensor(out=t["qxi"][:, j, n * 128:(n + 1) * 128],
                                                       in0=t["qr"][:, j, n * 128:(n + 1) * 128],
                                                       in1=t["xi"][:, h, :], op=ALU.mult),
                      reads=[b["qr"], b["xi"]], writes=[b["qxi"]])

    def ret_mixer(s, blk):
        c, cx, nc, t, b, d = s.c, s.cx, s.nc, s.t, s.b, s.d
        TB = c.TB
        act = lambda kk: t["hT"][:, kk, 0:TB]
        def head_gemms(h):
            def epiq(f, ps, pbuf):
                cx.op("act", lambda e: e.activation(out=t["qT"][:, f, :], in_=ps, func=AF.Copy), reads=[pbuf], writes=[b["qT"]])
            s.gemm(d["w_in"], 0, c.DC, h * 256, 256, act, TB, epiq, s.hTb)

            def epig(f, ps, pbuf):
                cx.op("act", lambda e: e.activation(out=t["sg"][:, h % 2, f, :], in_=ps, func=AF.Silu), reads=[pbuf], writes=[b["sg"][h % 2]])
            s.gemm(d["w_in"], 0, c.DC, 3 * c.TOK + h * 256, 256, act, TB, epig, s.hTb)

        head_gemms(0)
        for h in range(c.RH):
            if not s.planning:
                s.head_pre(h, blk)
            if h + 1 < c.RH:
                head_gemms(h + 1)
            if s.planning:
                continue
            for n in range(c.NTC):
                sl = slice(n * 128, (n + 1) * 128)
                bi = s.banks(1)[0]
                cx.acquire("pe", reads=[b["krb"], b["qr"]], writes=[s.psb[bi]])
                for j in range(2):
                    ins = nc.tensor.matmul(s.ps[bi][:, 0:128], lhsT=t["krb"][:, j, sl], rhs=t["qr"][:, j, sl],
                                           start=(j == 0), stop=(j == 1))
                cx.release("pe", ins, reads=[b["krb"], b["qr"]], writes=[s.psb[bi]])
                cx.op("dve", lambda e: e.scalar_tensor_tensor(out=t["sT"][:, n, :], in0=s.ps[bi][:, 0:128], scalar=1.0 / 16.0,
                                                              in1=t["maskT"][:, h, :], op0=ALU.mult, op1=ALU.mult),
                      reads=[s.psb[bi], b["maskT"]], writes=[b["sT"][n]])
                cx.op("act", lambda e: e.activation(out=t["stb"][:], in_=t["state"][:, h, :, :], func=AF.Copy),
                      reads=[b["state"][h]], writes=[b["stb"]])
                bo = s.banks(1)[0]
                rd = [b["vtm"], b["sT"][n], b["stb"], b["qxi"]]
                cx.acquire("pe", reads=rd, writes=[s.psb[bo]])
                for e_ in range(2):
                    es = slice(e_ * 128, (e_ + 1) * 128)
                    nc.tensor.matmul(s.ps[bo][:, es], lhsT=t["vtm"][:, n, es], rhs=t["sT"][:, n, :], start=True, stop=False)
                    for j in range(2):
                        ins = nc.tensor.matmul(s.ps[bo][:, es], lhsT=t["stb"][:, j, es], rhs=t["qxi"][:, j, sl],
                                               start=False, stop=(j == 1))
                cx.release("pe", ins, reads=rd, writes=[s.psb[bo]])
                for e_ in range(2):
                    cx.op("act", lambda e: e.activation(out=t["o"][:, e_, sl], in_=s.ps[bo][:, e_ * 128:(e_ + 1) * 128],
                                                        func=AF.Copy), reads=[s.psb[bo]], writes=[b["o"][n]])
                s.state_update(h, n)
            for e_ in range(2):
                cx.op("act", lambda e: e.activation(out=t["osq"][:, e_, :], in_=t["o"][:, e_, :], func=AF.Square),
                      reads=b["o"], writes=[b["osq"]])
            bi = s.banks(1)[0]
            cx.acquire("pe", reads=[b["osq"], b["const"]], writes=[s.psb[bi]])
            for e_ in range(2):
                ins = nc.tensor.matmul(s.ps[bi][:, 0:TB], lhsT=t["ones"][:], rhs=t["osq"][:, e_, :], start=(e_ == 0), stop=(e_ == 1))
            cx.release("pe", ins, reads=[b["osq"], b["const"]], writes=[s.psb[bi]])
            r = t["rstd"]
            cx.op("dve", lambda e: e.tensor_scalar(out=r[:], in0=s.ps[bi][:, 0:TB], scalar1=1.0 / 256.0, scalar2=EPS,
                                                   op0=ALU.mult, op1=ALU.add), reads=[s.psb[bi]], writes=[b["rstd"]])
            cx.op("act", lambda e: e.activation(out=r[:], in_=r[:], func=AF.Sqrt), reads=[b["rstd"]], writes=[b["rstd"]])
            cx.op("dve", lambda e: e.reciprocal(out=r[:], in_=r[:]), reads=[b["rstd"]], writes=[b["rstd"]])
            for e_ in range(2):
                cx.op("dve", lambda e: e.tensor_tensor(out=t["ra"][:], in0=t["o"][:, e_, :], in1=r[:], op=ALU.mult),
                      reads=b["o"] + [b["rstd"]], writes=[b["ra"]])
                cx.op("dve", lambda e: e.tensor_tensor(out=t["catT"][:, h * 2 + e_, :], in0=t["ra"][:], in1=t["sg"][:, h % 2, e_, :],
                                                       op=ALU.mult), reads=[b["ra"], b["sg"][h % 2]], writes=[b["catT"]])

    def out_and_ffn(s, col0, gi_ffn):
        c, cx, nc, t, b, d = s.c, s.cx, s.nc, s.t, s.b, s.d
        TB = c.TB
        if not s.planning:
            s.load_x(col0, TB)

        def epi_res(f, ps, pbuf):
            cx.op("dve", lambda e: e.tensor_tensor(out=t["xblk"][:, f, :], in0=t["xblk"][:, f, :], in1=ps, op=ALU.add),
                  reads=[pbuf, b["xblk"][f // 8]], writes=[b["xblk"][f // 8]])
        s.gemm(d["w_out"], 0, c.DC, 0, c.D, lambda kk: t["catT"][:, kk, 0:TB], TB, epi_res, [b["catT"]])
        s.norm(t["xblk"], b["xblk"], TB, gi_ffn, t["hT"], b["hT"])
        aT, aTb = t["catT"], b["catT"]
        for jb in range(c.NFB):
            def epi_up(f, ps, pbuf):
                cx.op("act", lambda e: e.activation(out=t["relu"][:], in_=ps, func=AF.Relu), reads=[pbuf], writes=[b["relu"]])
                cx.op("act", lambda e: e.activation(out=aT[:, f, :], in_=t["relu"][:], func=AF.Square),
                      reads=[b["relu"]], writes=[aTb])
            s.gemm(d["w_up"], 0, c.DC, jb * c.FB * 128, c.FB * 128, lambda kk: t["hT"][:, kk, 0:TB], TB, epi_up, s.hTb)
            s.gemm(d["w_down"], jb * c.FB, c.FB, 0, c.D, lambda kk: aT[:, kk, 0:TB], TB, epi_res, [aTb])

    def store_x(s, blk, final_gi=None):
        if s.planning:
            return
        c, cx, t, b, d = s.c, s.cx, s.t, s.b, s.d
        TB = c.TB
        dst = d["xT_out"].rearrange("(k p) t -> p k t", p=128)
        for k in range(0, c.DC, 8):
            cx.dma("sp", dst[:, k:k + 8, blk * TB:(blk + 1) * TB], t["xblk"][:, k:k + 8, :], reads=[b["xblk"]])
        if final_gi is not None:
            s.norm_stats(t["xblk"], b["xblk"], TB)
            dstn = d["xnT_out"].rearrange("(k p) t -> p k t", p=128)
            hc = c.DC // 2
            for half in range(2):
                for kk in range(hc):
                    kc = half * hc + kk
                    cx.op("dve", lambda e: e.scalar_tensor_tensor(
                        out=t["xnv"][:, kk, :], in0=t["xblk"][:, kc, :], scalar=t["gcol"][:, final_gi * c.DC + kc:final_gi * c.DC + kc + 1],
                        in1=t["rstd"][:], op0=ALU.mult, op1=ALU.mult), reads=[b["xblk"], b["rstd"], b["const"]], writes=[b["hT"]])
                cx.dma("sp", dstn[:, half * hc:(half + 1) * hc, blk * TB:(blk + 1) * TB], t["xnv"][:], reads=[b["hT"]])

    def norm_stats(s, src, srcb, T):
        c, cx, nc, t, b = s.c, s.cx, s.nc, s.t, s.b
        dst, dstb = t["hT"], b["hT"]
        for kc in range(c.DC):
            cx.op("act", lambda e: e.activation(out=dst[:, kc, 0:T], in_=src[:, kc, 0:T], func=AF.Square),
                  reads=[srcb], writes=[dstb])
        bi = s.banks(1)[0]
        cx.acquire("pe", reads=[dstb, b["const"]], writes=[s.psb[bi]])
        for kc in range(c.DC):
            ins = nc.tensor.matmul(s.ps[bi][:, 0:T], lhsT=t["ones"][:], rhs=dst[:, kc, 0:T], start=(kc == 0), stop=(kc == c.DC - 1))
        cx.release("pe", ins, reads=[dstb, b["const"]], writes=[s.psb[bi]])
        r = t["rstd"]
        cx.op("dve", lambda e: e.tensor_scalar(out=r[:, 0:T], in0=s.ps[bi][:, 0:T], scalar1=1.0 / c.D, scalar2=EPS,
                                               op0=ALU.mult, op1=ALU.add), reads=[s.psb[bi]], writes=[b["rstd"]])
        cx.op("act", lambda e: e.activation(out=r[:, 0:T], in_=r[:, 0:T], func=AF.Sqrt), reads=[b["rstd"]], writes=[b["rstd"]])
        cx.op("dve", lambda e: e.reciprocal(out=r[:, 0:T], in_=r[:, 0:T]), reads=[b["rstd"]], writes=[b["rstd"]])

    def body(s):
        c, cx, nc, t, b, d = s.c, s.cx, s.nc, s.t, s.b, s.d
        A = s.kind == "A"
        s.mem_kv()
        if A:
            if not s.planning:
                srcx = d["xT"].rearrange("(k p) t -> p k t", p=128)
                cx.dma("sp", t["xh"][:], srcx[:, :, 0:HALO], writes=[b["xh"]])
            s.load_x(HALO, c.TB)
            s.norm(t["xh"], b["xh"], HALO, 0, t["hTh"], b["hTh"])

            def epih(f, ps, pbuf):
                cx.op("act", lambda e: e.activation(out=t["uhalo"][:, f, :], in_=ps, func=AF.Copy), reads=[pbuf], writes=[b["uhalo"]])
            s.gemm(d["w_in"], 0, c.DC, 0, c.TOK, lambda kk: t["hTh"][:, kk, 0:HALO], HALO, epih, [b["hTh"]])
            for blk in range(c.NBLK):
                col0 = HALO + blk * c.TB
                if blk > 0:
                    s.load_x(col0, c.TB)
                s.norm(t["xblk"], b["xblk"], c.TB, 0, t["hT"], b["hT"])
                cx.inherit(s.tmpbufs, [b["xblk"]])
                s.mem_attn(d["w_in"], c.TOK, interleave=True)
                s.pool_mixer(blk)
                while getattr(s, "gen", None) is not None:
                    s.pump()
                cx.inherit([b["xblk"]], s.tmpbufs)
                s.out_and_ffn(col0, 1)
                s.store_x(blk)
            if not s.planning:
                for h in range(c.RH):
                    cx.op("dve", lambda e: e.memset(t["state"][:, h, :, :], 0.0), writes=[b["state"][h]])
            xo = d["xT_out"].rearrange("(k p) t -> p k t", p=128)
            for blk in range(c.NBLK):
                if not s.planning:
                    for key in cx.dslots["sp"]:
                        if cx.cnt[key] > 0:
                            cx._need("sp", (key, cx.cnt[key]))
                    for k in range(0, c.DC, 8):
                        cx.dma("sp", t["xblk"][:, k:k + 8, :], xo[:, k:k + 8, blk * c.TB:(blk + 1) * c.TB], writes=[b["xblk"]])
                s.norm(t["xblk"], b["xblk"], c.TB, 3, t["hT"], b["hT"])
                cx.inherit(s.tmpbufs, [b["xblk"]])
                s.rope_tables(blk, False)
                s.ret_pass1(blk)
                cx.inherit([b["xblk"]], s.tmpbufs)
            if not s.planning:
                lo = d["Lout"].rearrange("(h j p) e -> p h j e", p=128, j=2)
                cx.dma("sp", lo, t["state"][:], reads=b["state"])
        else:
            s.load_x(HALO, c.TB)
            s.norm(t["xblk"], b["xblk"], c.TB, 0, t["hT"], b["hT"])
            if not s.planning:
                cx.inherit(s.LstB, [b["catT"]])
                la = d["Lall"].rearrange("(r h j p) e -> p r h j e", p=128, j=2, h=c.RH)
                for h in range(c.RH):
                    Lt, Lb = s.LstT[h % 2], s.LstB[h % 2]
                    for j in range(2):
                        cx.dma("sp", Lt[:, :, j, :], la[:, :, h, j, :], writes=[Lb])
                    st = t["state"][:, h, :, :]
                    for r in range(c.NCORES):
                        wcol = t["wS"][:, r * c.RH + h:r * c.RH + h + 1]
                        if r == 0:
                            cx.op("dve", lambda e: e.tensor_scalar(out=st, in0=Lt[:, 0, :, :], scalar1=wcol, scalar2=None,
                                                                   op0=ALU.mult), reads=[Lb, b["const"]], writes=[b["state"][h]])
                        else:
                            cx.op("dve", lambda e: e.scalar_tensor_tensor(out=st, in0=Lt[:, r, :, :], scalar=wcol, in1=st,
                                                                          op0=ALU.mult, op1=ALU.add),
                                  reads=[Lb, b["const"], b["state"][h]], writes=[b["state"][h]])
                cx.inherit([b["catT"]], s.LstB)
            for blk in range(c.NBLK):
                col0 = HALO + blk * c.TB
                if blk > 0:
                    s.load_x(col0, c.TB)
                    s.norm(t["xblk"], b["xblk"], c.TB, 0, t["hT"], b["hT"])
                cx.inherit(s.tmpbufs, [b["xblk"]])
                s.rope_tables(blk, True)
                s.ret_mixer(blk)
                cx.inherit(s.membufs, s.tmpbufs)
                s.mem_attn(d["w_in"], 4 * c.TOK)
                cx.inherit([b["xblk"]], s.tmpbufs)
                s.out_and_ffn(col0, 1)
                s.store_x(blk, final_gi=3)


_PROGS = {}


def _ret_consts(cfg):
    H, C = cfg.RH, 128
    log_g = np.log1p(-(np.float32(2.0) ** (-5.0 - np.arange(H, dtype=np.float32)))).astype(np.float32)
    idx = np.arange(C, dtype=np.float32)
    diff = idx[:, None] - idx[None, :]
    decay_in = np.where(diff >= 0, np.exp(log_g[:, None, None] * np.maximum(diff, 0.0)), 0.0).astype(np.float32)
    xi = np.exp(log_g[None, :] * (idx + 1.0)[:, None]).astype(np.float32)
    zeta = np.exp(log_g[None, :] * (C - 1.0 - idx)[:, None]).astype(np.float32)
    cd = np.exp(log_g.astype(np.float64) * C)
    maskT = np.ascontiguousarray(decay_in.transpose(2, 0, 1)).reshape(C, H * C)
    xi_rep = np.ascontiguousarray(np.broadcast_to(xi.T[None, :, :], (128, H, C))).reshape(128, H * C)
    zeta16 = np.ascontiguousarray(zeta / 16.0).astype(np.float32)
    half = 128
    inv32 = (np.float32(10000.0) ** (-np.arange(half, dtype=np.float32) / np.float32(half))).astype(np.float32)
    invf = (inv32.astype(np.float64) / (2 * np.pi)).astype(np.float32).reshape(128, 1)
    return dict(maskT=maskT.astype(np.float32), xi=xi_rep.astype(np.float32), zeta16=zeta16, invf=invf), cd, log_g


def _col(g, DC):
    return np.ascontiguousarray(g.reshape(DC, 128).T)


def run_model(cfg, x, mem, positions, mix_norm_g, w_in_pool, w_pool_grp, pool_scale, w_out_pool,
              w_in_ret, w_out_ret, mem_norm_g, w_mem_kv, ffn_norm_g, w_up, w_down, final_norm_g, depth):
    NCO, TL, D = cfg.NCORES, cfg.TL, cfg.D
    consts, cd, log_g = _ret_consts(cfg)
    cds = [float(v) for v in cd]
    key = (cfg.TL, cfg.TB, cfg.DFF, cfg.NCORES)
    if key not in _PROGS:
        _PROGS[key] = (Prog(cfg, "A", cds, False), Prog(cfg, "A", cds, True), Prog(cfg, "B", cds, True))
    pA0, pA1, pB = _PROGS[key]
    cores = list(range(NCO))
    xT = np.ascontiguousarray(np.asarray(x)[0].T)
    memT = np.ascontiguousarray(np.asarray(mem)[0].T)
    pos = np.asarray(positions)[0].astype(np.int32)
    ident = np.eye(128, dtype=np.float32)
    wS = np.zeros((NCO, NCO, cfg.RH), np.float64)
    for cc in range(NCO):
        for r in range(cc):
            wS[cc, r, :] = cd ** (cfg.NCH * (cc - 1 - r))
    invc = np.zeros((NCO, 4, 16), np.float32)
    for g, w in enumerate((2, 4, 8, 16)):
        for cc in range(NCO):
            tg = cc * TL + np.arange(16)
            invc[cc, g] = 1.0 / np.minimum(tg + 1, w)

    def halo_x(xT_full, cc):
        out = np.zeros((D, HALO + TL), np.float32)
        out[:, HALO:] = xT_full[:, cc * TL:(cc + 1) * TL]
        if cc > 0:
            out[:, :HALO] = xT_full[:, cc * TL - HALO:cc * TL]
        return out

    def common(cc, xT_full, gains):
        m = dict(xT=halo_x(xT_full, cc), ident=ident,
                 gcol=np.ascontiguousarray(np.concatenate([_col(np.asarray(g), cfg.DC) for g in gains], axis=1)),
                 pos=np.ascontiguousarray(np.broadcast_to(pos[None, cc * TL:(cc + 1) * TL], (128, TL))))
        m.update(consts)
        return m

    out = None
    for j in range(depth // 2):
        ip, ir = 2 * j, 2 * j + 1
        ins = []
        for cc in cores:
            m = common(cc, xT, [mix_norm_g[ip], ffn_norm_g[ip], mem_norm_g, mix_norm_g[ir]])
            m.update(w_in=np.asarray(w_in_pool[j]), w_grp=np.asarray(w_pool_grp[j]).reshape(4 * cfg.G, cfg.G),
                     pscale=_col(np.asarray(pool_scale[j]), cfg.TOKC), w_out=np.asarray(w_out_pool[j]),
                     w_up=np.asarray(w_up[ip]), w_down=np.asarray(w_down[ip]), w_in2=np.asarray(w_in_ret[j]),
                     invc=np.ascontiguousarray(np.broadcast_to(invc[cc].reshape(1, 64), (128, 64))))
            if j == 0:
                m.update(memT=memT, w_mem_kv=np.asarray(w_mem_kv))
            else:
                m.update(KT_i=memkv[0], V_i=memkv[1])
            ins.append(m)
        res = run_bass_kernel_spmd((pA0 if j == 0 else pA1).nc, ins, core_ids=cores).results
        if j == 0:
            memkv = (res[0]["KT_o"], res[0]["V_o"])
        xT = np.concatenate([r["xT_out"] for r in res], axis=1)
        Lall = np.concatenate([r["Lout"] for r in res], axis=0)
        kvs = [(r["kz_o"], r["v_o"], r["kr_o"]) for r in res]
        ins = []
        for cc in cores:
            m = common(cc, xT, [mix_norm_g[ir], ffn_norm_g[ir], mem_norm_g, final_norm_g])
            m.update(w_in=np.asarray(w_in_ret[j]), w_out=np.asarray(w_out_ret[j]), w_up=np.asarray(w_up[ir]),
                     w_down=np.asarray(w_down[ir]), Lall=Lall, KT_i=memkv[0], V_i=memkv[1], kz_i=kvs[cc][0], v_i=kvs[cc][1], kr_i=kvs[cc][2],
                     wS=np.ascontiguousarray(np.broadcast_to(wS[cc].reshape(1, -1), (128, NCO * cfg.RH))).astype(np.float32))
            ins.append(m)
        res = run_bass_kernel_spmd(pB.nc, ins, core_ids=cores).results
        xT = np.concatenate([r["xT_out"] for r in res], axis=1)
        out = np.concatenate([r["xnT_out"] for r in res], axis=1)
    return np.ascontiguousarray(out.T)[None].astype(np.float32), np.ascontiguousarray(xT.T)[None]


def kernel(x, mem, positions, mix_norm_g, w_in_pool, w_pool_grp, pool_scale, w_out_pool,
           w_in_ret, w_out_ret, mem_norm_g, w_mem_kv, ffn_norm_g, w_up, w_down, final_norm_g):
    cfg = Cfg()
    out, _ = run_model(cfg, x, mem, positions, mix_norm_g, w_in_pool, w_pool_grp, pool_scale, w_out_pool,
                       w_in_ret, w_out_ret, mem_norm_g, w_mem_kv, ffn_norm_g, w_up, w_down, final_norm_g, 4)
    return out
```
